# Optimizing a Trainium2 kernel written in Bass

```python
import jax, jax.numpy as jnp
from jax import lax
import numpy as np

D_MODEL = 2048
BATCH = 4
SEQ = 2048
DEPTH = 2

EPS = 1e-6
NEG = -1e30
HEAD_DIM = 128
ROT_DIM = HEAD_DIM // 4
ROPE_THETA = 500000.0
DILATED_PATTERNS = ((128, 1), (512, 4), (2048, 16))
ATT_GROUPS = len(DILATED_PATTERNS)
HEADS_PER_GROUP = D_MODEL // (2 * HEAD_DIM)
ATT_QKV = ATT_GROUPS * HEADS_PER_GROUP * HEAD_DIM
ATT_OUT = HEADS_PER_GROUP * HEAD_DIM
POOL_WINDOWS = (2, 4, 8, 16)
POOL_GROUPS = len(POOL_WINDOWS)
POOL_WIDTH = D_MODEL // 2
POOL_CH = POOL_WIDTH // POOL_GROUPS
SGU_WIDTH = D_MODEL // 2
SGU_GROUPS = 4
SGU_CH = SGU_WIDTH // SGU_GROUPS
CHUNK = 128
CONV_WIDTH = D_MODEL // 2
CONV_K = 31
EVEN_COLS = (POOL_WIDTH, POOL_WIDTH, ATT_QKV, ATT_QKV, ATT_QKV, ATT_OUT)
ODD_COLS = (SGU_WIDTH, SGU_WIDTH, SGU_WIDTH, CONV_WIDTH, CONV_WIDTH, CONV_WIDTH)
EVEN_IN = sum(EVEN_COLS)
ODD_IN = sum(ODD_COLS)
MIX_OUT = POOL_WIDTH + ATT_OUT
N_EVEN = (DEPTH + 1) // 2
N_ODD = DEPTH // 2

kernel_name = "hybrid_pool_dilattn_sgu_conv"


def _split_points(cols):
    return [int(c) for c in np.cumsum(cols)[:-1]]


def rmsnorm(x, g):
    xf = x.astype(jnp.float32)
    y = xf * lax.rsqrt(jnp.mean(xf * xf, axis=-1, keepdims=True) + EPS) * g.astype(jnp.float32)
    return y.astype(x.dtype)


def layernorm(x, g, b):
    xf = x.astype(jnp.float32)
    mu = jnp.mean(xf, axis=-1, keepdims=True)
    var = jnp.mean(jnp.square(xf - mu), axis=-1, keepdims=True)
    y = (xf - mu) * lax.rsqrt(var + EPS) * g.astype(jnp.float32) + b.astype(jnp.float32)
    return y.astype(x.dtype)


def partial_rope(t, cos, sin):
    tf = t.astype(jnp.float32)
    half = ROT_DIM // 2
    t1, t2 = tf[..., :half], tf[..., half:ROT_DIM]
    c, s = cos[None, :, None, :], sin[None, :, None, :]
    out = jnp.concatenate([t1 * c - t2 * s, t2 * c + t1 * s, tf[..., ROT_DIM:]], axis=-1)
    return out.astype(t.dtype)


def causal_pool_mixer(xa, pool_w, pool_scale):
    B, S, _ = xa.shape
    xg = xa.reshape(B, S, POOL_GROUPS, POOL_CH).astype(jnp.float32)
    csp = jnp.concatenate([jnp.zeros((B, 1, POOL_GROUPS, POOL_CH), jnp.float32),
                           jnp.cumsum(xg, axis=1)], axis=1)
    t = jnp.arange(S)
    outs = []
    for g, w in enumerate(POOL_WINDOWS):
        upper = csp[:, 1:, g]
        lower = jnp.concatenate([jnp.zeros((B, w - 1, POOL_CH), jnp.float32),
                                 csp[:, :S + 1 - w, g]], axis=1)
        count = jnp.minimum(t + 1, w).astype(jnp.float32)[None, :, None]
        outs.append((upper - lower) / count - xg[:, :, g])
    pooled = jnp.stack(outs, axis=2).astype(xa.dtype)
    mixed = jnp.einsum('bsgc,gcd->bsgd', pooled, pool_w)
    return mixed.reshape(B, S, POOL_WIDTH) * pool_scale


def dilated_group(q, k, v, dilation, span):
    B, S, H, E = q.shape
    L = S // dilation
    nb = -(-L // span)
    Lp = nb * span

    def to_blocks(t):
        t = t.reshape(B, L, dilation, H, E)
        t = jnp.pad(t, ((0, 0), (0, Lp - L), (0, 0), (0, 0), (0, 0)))
        return t.reshape(B, nb, span, dilation, H, E)

    def with_prev(t):
        prev = jnp.pad(t, ((0, 0), (1, 0), (0, 0), (0, 0), (0, 0), (0, 0)))[:, :-1]
        return jnp.concatenate([prev, t], axis=2)

    qb = to_blocks(q)
    kk = with_prev(to_blocks(k))
    vv = with_prev(to_blocks(v))
    s = jnp.einsum('bnqrhe,bnkrhe->bnrhqk', qb, kk,
                   preferred_element_type=jnp.float32) * (HEAD_DIM ** -0.5)
    qi = jnp.arange(span)[:, None]
    kj = jnp.arange(2 * span)[None, :] - span
    dist = qi - kj
    blk = jnp.arange(nb)[:, None, None]
    valid = (dist >= 0)[None] & (dist <= span)[None] & (blk * span + kj[None] >= 0)
    s = jnp.where(valid[None, :, None, None], s, NEG)
    m = jnp.max(s, axis=-1, keepdims=True)
    p = jnp.exp(s - m)
    den = jnp.sum(p, axis=-1)
    o = jnp.einsum('bnrhqk,bnkrhe->bnqrhe', p.astype(vv.dtype), vv,
                   preferred_element_type=jnp.float32)
    den_t = jnp.transpose(den, (0, 1, 4, 2, 3))
    o = o / den_t[..., None]
    lse = jnp.transpose(m[..., 0], (0, 1, 4, 2, 3)) + jnp.log(den_t)
    o = o.reshape(B, Lp, dilation, H, E)[:, :L].reshape(B, S, H, E)
    lse = lse.reshape(B, Lp, dilation, H)[:, :L].reshape(B, S, H)
    return o, lse


def dilated_attention(q, k, v, cos, sin):
    B, S, _ = q.shape
    shp = (B, S, ATT_GROUPS * HEADS_PER_GROUP, HEAD_DIM)
    q = partial_rope(q.reshape(shp), cos, sin).reshape(B, S, ATT_GROUPS, HEADS_PER_GROUP, HEAD_DIM)
    k = partial_rope(k.reshape(shp), cos, sin).reshape(B, S, ATT_GROUPS, HEADS_PER_GROUP, HEAD_DIM)
    v = v.reshape(B, S, ATT_GROUPS, HEADS_PER_GROUP, HEAD_DIM)
    outs, lses = [], []
    for g, (window, dilation) in enumerate(DILATED_PATTERNS):
        o_g, lse_g = dilated_group(q[:, :, g], k[:, :, g], v[:, :, g], dilation, window // dilation)
        outs.append(o_g)
        lses.append(lse_g)
    wts = jax.nn.softmax(jnp.stack(lses, axis=0), axis=0)
    o = jnp.sum(wts[..., None] * jnp.stack(outs, axis=0), axis=0)
    return o.reshape(B, S, ATT_OUT).astype(q.dtype)


def chunked_sgu(u, v, g, b, w_s, b_s):
    B, S, _ = v.shape
    vn = layernorm(v, g, b).reshape(B, S // CHUNK, CHUNK, SGU_GROUPS, SGU_CH)
    mask = jnp.tril(jnp.ones((CHUNK, CHUNK), w_s.dtype))
    s = jnp.einsum('hij,bnjhc->bnihc', w_s * mask[None], vn) + b_s.T[None, None, :, :, None]
    return u * s.reshape(B, S, SGU_WIDTH)


def causal_depthwise_conv(x, w, b):
    out = lax.conv_general_dilated(x, w[:, None, :].astype(x.dtype), window_strides=(1,),
                                   padding=[(CONV_K - 1, 0)],
                                   dimension_numbers=('NWC', 'WIO', 'NWC'),
                                   feature_group_count=x.shape[-1])
    return out + b


def even_mixer(h, w_in, pool_w, pool_scale, w_out, cos, sin):
    z = h @ w_in
    a_in, a_gate, q, k, v, b_gate = jnp.split(z, _split_points(EVEN_COLS), axis=-1)
    ya = causal_pool_mixer(a_in, pool_w, pool_scale) * jax.nn.silu(a_gate)
    yb = dilated_attention(q, k, v, cos, sin) * jax.nn.silu(b_gate)
    return jnp.concatenate([ya, yb], axis=-1) @ w_out


def odd_mixer(h, w_in, sgu_g, sgu_b, sgu_w, sgu_bias, conv_w, conv_b, cn_g, cn_b, w_out):
    z = h @ w_in
    u, v, c_gate, d_val, d_glu, d_gate = jnp.split(z, _split_points(ODD_COLS), axis=-1)
    yc = chunked_sgu(u, v, sgu_g, sgu_b, sgu_w, sgu_bias) * jax.nn.silu(c_gate)
    d = d_val * jax.nn.sigmoid(d_glu)
    d = causal_depthwise_conv(d, conv_w, conv_b)
    d = jax.nn.silu(layernorm(d, cn_g, cn_b))
    yd = d * jax.nn.silu(d_gate)
    return jnp.concatenate([yc, yd], axis=-1) @ w_out


def setup_inputs(seed: int = 0) -> dict:
    key = jax.random.key(seed)
    ks = jax.random.split(key, 20)
    f32 = jnp.float32

    def nrm(k, shape, scale):
        return jax.random.normal(k, shape, f32) * scale

    def gain(k, shape):
        return 1.0 + 0.05 * jax.random.normal(k, shape, f32)

    return {
        "x": jax.random.normal(ks[0], (BATCH, SEQ, D_MODEL), f32),
        "e_pre_norm": gain(ks[1], (N_EVEN, D_MODEL)),
        "e_w_in": nrm(ks[2], (N_EVEN, D_MODEL, EVEN_IN), D_MODEL ** -0.5),
        "e_pool_w": nrm(ks[3], (N_EVEN, POOL_GROUPS, POOL_CH, POOL_CH), POOL_CH ** -0.5),
        "e_pool_scale": gain(ks[4], (N_EVEN, POOL_WIDTH)),
        "e_w_out": nrm(ks[5], (N_EVEN, MIX_OUT, D_MODEL), MIX_OUT ** -0.5),
        "e_post_norm": gain(ks[6], (N_EVEN, D_MODEL)),
        "o_pre_norm": gain(ks[7], (N_ODD, D_MODEL)),
        "o_w_in": nrm(ks[8], (N_ODD, D_MODEL, ODD_IN), D_MODEL ** -0.5),
        "o_sgu_norm_g": gain(ks[9], (N_ODD, SGU_WIDTH)),
        "o_sgu_norm_b": nrm(ks[10], (N_ODD, SGU_WIDTH), 0.02),
        "o_sgu_w": nrm(ks[11], (N_ODD, SGU_GROUPS, CHUNK, CHUNK), CHUNK ** -0.5),
        "o_sgu_b": gain(ks[12], (N_ODD, SGU_GROUPS, CHUNK)),
        "o_conv_w": nrm(ks[13], (N_ODD, CONV_K, CONV_WIDTH), CONV_K ** -0.5),
        "o_conv_b": nrm(ks[14], (N_ODD, CONV_WIDTH), 0.02),
        "o_conv_norm_g": gain(ks[15], (N_ODD, CONV_WIDTH)),
        "o_conv_norm_b": nrm(ks[16], (N_ODD, CONV_WIDTH), 0.02),
        "o_w_out": nrm(ks[17], (N_ODD, MIX_OUT, D_MODEL), MIX_OUT ** -0.5),
        "o_post_norm": gain(ks[18], (N_ODD, D_MODEL)),
    }


def reference(x, e_pre_norm, e_w_in, e_pool_w, e_pool_scale, e_w_out, e_post_norm,
              o_pre_norm, o_w_in, o_sgu_norm_g, o_sgu_norm_b, o_sgu_w, o_sgu_b,
              o_conv_w, o_conv_b, o_conv_norm_g, o_conv_norm_b, o_w_out, o_post_norm):
    S = x.shape[1]
    pos = jnp.arange(S, dtype=jnp.float32)
    inv_freq = jnp.power(ROPE_THETA, -jnp.arange(0, ROT_DIM, 2, dtype=jnp.float32) / ROT_DIM)
    ang = pos[:, None] * inv_freq[None, :]
    cos, sin = jnp.cos(ang), jnp.sin(ang)
    for i in range(DEPTH):
        j = i // 2
        if i % 2 == 0:
            h = rmsnorm(x, e_pre_norm[j])
            y = even_mixer(h, e_w_in[j], e_pool_w[j], e_pool_scale[j], e_w_out[j], cos, sin)
            x = x + rmsnorm(y, e_post_norm[j])
        else:
            h = rmsnorm(x, o_pre_norm[j])
            y = odd_mixer(h, o_w_in[j], o_sgu_norm_g[j], o_sgu_norm_b[j], o_sgu_w[j], o_sgu_b[j],
                          o_conv_w[j], o_conv_b[j], o_conv_norm_g[j], o_conv_norm_b[j], o_w_out[j])
            x = x + rmsnorm(y, o_post_norm[j])
    return x
```

```python
import contextlib
import os
DBGV = os.environ.get('DBGV', '')
import numpy as np
import concourse.bass as bass
import concourse.mybir as mybir
from concourse.bass_utils import run_bass_kernel_spmd

F32 = mybir.dt.float32
BF16 = mybir.dt.bfloat16
AF = mybir.ActivationFunctionType
ALU = mybir.AluOpType
AX = mybir.AxisListType

D = 2048
EPS = 1e-6
NCORES = 8
ARENA = 51200


class Sched:
    ENG = ("pe", "act", "dve", "pool", "sp")

    def __init__(self, nc, stack):
        self.nc = nc
        self.q = {e: [] for e in self.ENG}
        self.sem = {}
        for e in self.ENG:
            self.sem[e] = stack.enter_context(nc.semaphore("s_" + e))
        self.cnt = {e: 0 for e in self.ENG}
        self.dcnt = {}
        self.waited = {e: {} for e in self.ENG}
        self.res = {}
        self.pending = {e: [] for e in self.ENG}
        self.stack = stack

    def dma_sem(self, key):
        if key not in self.sem:
            self.sem[key] = self.stack.enter_context(self.nc.semaphore("d_%d" % len(self.sem)))
            self.dcnt[key] = 0
        return key

    def _deps(self, reads, writes):
        deps = []
        for r in reads:
            st = self.res.get(r)
            if st and st["w"] is not None:
                deps.append(st["w"])
        for w in writes:
            st = self.res.get(w)
            if st:
                if st["w"] is not None:
                    deps.append(st["w"])
                deps.extend(st["r"])
        return deps

    def _emit_waits(self, eng, deps):
        need = {}
        for tok in deps:
            k, v = tok[0], tok[1]
            if k == eng and eng == "pe":
                continue
            assert v is not None, "unresolved token used as dependency"
            if v > need.get(k, 0):
                need[k] = v
        for k, v in need.items():
            if self.waited[eng].get(k, 0) >= v:
                continue
            self.waited[eng][k] = v
            sem = self.sem[k]
            if DBGV:
                print("  [%s] WAIT %s >= %d" % (eng, k, v))
            self.q[eng].append(lambda e, sem=sem, v=v: e.wait_ge(sem, v))

    def _update(self, tok, reads, writes):
        for r in reads:
            st = self.res.setdefault(r, {"w": None, "r": []})
            st["r"].append(tok)
            if len(st["r"]) > 24:
                st["r"] = st["r"][-24:] if False else st["r"]
        for w in writes:
            self.res[w] = {"w": tok, "r": []}

    def op(self, eng, fn, reads=(), writes=(), signal=True):
        deps = self._deps(reads, writes)
        self._emit_waits(eng, deps)
        if DBGV:
            print("  [%s] OP sig=%s -> %s  r=%s w=%s" % (eng, signal, self.cnt[eng] + (1 if signal else 0), list(reads), list(writes)))
        if signal:
            self.cnt[eng] += 1
            v = self.cnt[eng]
            tok = [eng, v]
            for p in self.pending[eng]:
                p[1] = v
            self.pending[eng] = []
            sem = self.sem[eng]
            self.q[eng].append(lambda e, fn=fn, sem=sem: fn(e).then_inc(sem, 1))
        else:
            tok = [eng, None]
            self.pending[eng].append(tok)
            self.q[eng].append(lambda e, fn=fn: fn(e))
        self._update(tok, reads, writes)
        return tok

    def dma(self, eng, out, in_, key, reads=(), writes=()):
        self.dma_sem(key)
        deps = self._deps(reads, writes)
        self._emit_waits(eng, deps)
        self.dcnt[key] += 16
        if DBGV:
            print("  [%s] DMA %s -> %d r=%s w=%s" % (eng, key, self.dcnt[key], list(reads), list(writes)))
        tok = [key, self.dcnt[key]]
        sem = self.sem[key]
        self.q[eng].append(lambda e, out=out, in_=in_, sem=sem: e.dma_start(out=out, in_=in_).then_inc(sem, 16))
        self._update(tok, reads, writes)
        return tok

    def barrier(self, skip=()):
        for e in self.ENG:
            assert not self.pending[e], "unsignaled op pending on %s at barrier" % e
        toks = [[e, self.cnt[e]] for e in self.ENG if self.cnt[e] > 0]
        toks += [[k, v] for k, v in self.dcnt.items() if v > 0 and k not in skip]
        for e in self.ENG:
            self._emit_waits(e, toks)
        self.res = {k: {"w": self.res[k]["w"], "r": []} for k in skip if k in self.res}

    def emit(self):
        with self.nc.Block() as block:
            @block.tensor
            def _(e):
                for f in self.q["pe"]:
                    f(e)

            @block.scalar
            def _(e):
                for f in self.q["act"]:
                    f(e)

            @block.vector
            def _(e):
                for f in self.q["dve"]:
                    f(e)

            @block.gpsimd
            def _(e):
                for f in self.q["pool"]:
                    f(e)

            @block.sync
            def _(e):
                for f in self.q["sp"]:
                    f(e)


class Arena:
    def __init__(self, t, lo, hi):
        self.t, self.lo, self.hi, self.p = t, lo, hi, lo

    def f32(self, n):
        n = (n + 7) // 8 * 8
        assert self.p + n <= self.hi, "arena overflow %d + %d > %d" % (self.p, n, self.hi)
        ap = self.t[:, self.p:self.p + n]
        self.p += n
        return ap

    def bf(self, n):
        return self.f32((n + 1) // 2).bitcast(BF16)[:, 0:n]

    def sub(self):
        return Arena(self.t, self.p, self.hi)


QCH = [(896, 128), (1024, 512), (1536, 512)]
ACH0 = [(768, 256), (1024, 512), (1536, 512)]
ACH = [(0, 512), (512, 512), (1024, 512), (1536, 512)]
NSLOT = 4


def build_nc(stop=99):
    nc = bass.Bass("TRN2", target_bir_lowering=False)
    dt = lambda name, shape, kind="ExternalInput": nc.dram_tensor(name, shape, F32, kind=kind).ap()
    xw = dt("xw", [2048, D])
    wseq = dt("wseq", [144, 128, 2048])
    woseq = dt("woseq", [8, 128, 8192])
    gvec = dt("gvec", [4, 128, 2048])
    sguc = dt("sguc", [128, 2560])
    wsT_d = dt("wsT", [128, 512])
    pcol_d = dt("pcol", [128, 280])
    poolw_d = dt("poolw", [4, 128, 512])
    rope_d = dt("rope", [32, 4096])
    cbf_d = dt("cbf", [128, 872])
    cf32_d = dt("cf32", [128, 192])
    out = dt("out", [1024, D], kind="ExternalOutput")
    x1s = dt("x1s", [1152, D], kind="Internal")

    with contextlib.ExitStack() as stk:
        arena_t = stk.enter_context(nc.sbuf_tensor("arena", [128, ARENA], F32))
        PS = stk.enter_context(nc.psum_tensor("ps", [128, 4096], F32))
        S = Sched(nc, stk)
        top = Arena(arena_t, 0, ARENA)

        def bank(b, n=512, off=0):
            return PS[:, b * 512 + off: b * 512 + off + n]

        cbf = top.bf(872)
        cf32 = top.f32(192)
        pcol = top.f32(280)
        stat = top.f32(64)
        ring = [top.bf(2048).rearrange("p (k c) -> p k c", k=16) for _ in range(NSLOT)]
        ident = cbf[:, 0:128]
        ones_bf = cbf[:, 128:256]
        psw = cbf[0:32, 256:288]
        m_std = cbf[:, 288:544]
        m_bnd = cbf[:, 544:800]
        m_g2 = cbf[:, 800:872]
        invfix = cf32[:, 0:64]
        ones_f = cf32[:, 64:192]
        S.dma("pool", cbf, cbf_d, "cp", writes=["cbf"])
        S.dma("sp", cf32, cf32_d, "c", writes=["cf32"])
        S.dma("sp", pcol, pcol_d, "c", writes=["pcol"])

        def finish(dumps):
            for i, (src_ap, r0) in enumerate(dumps):
                S.dma("pool", out[r0:r0 + 128, 0:src_ap.shape[-1]], src_ap, "o")
            S._emit_waits("pool", [["o", S.dcnt["o"]]])
            print("op counts", S.cnt, S.dcnt)
            S.emit()
            return nc

        if stop == 0:
            return finish([(cbf[:, 0:128], 0), (cbf[:, 288:544], 128)])
        wstate = {"next": 0}

        def w_issue(upto):
            while wstate["next"] < min(upto, 144):
                i = wstate["next"]
                s = i % NSLOT
                S.dma("pool", ring[s].rearrange("p k c -> p (k c)"), wseq[i], ("w", s), writes=[("w", s)])
                wstate["next"] += 1

        wuse = {"i": 0}

        def w_take():
            i = wuse["i"]
            wuse["i"] += 1
            w_issue(i + 1)
            return i % NSLOT

        def w_done():
            w_issue(wuse["i"] + NSLOT - 1)

        accst = {"i": 0}
        ACC_BANKS = [0, 1, 6, 7]

        deferred = []
        fillers = []

        def flush_deferred():
            fs = list(deferred)
            del deferred[:]
            for f in fs:
                f()

        def inproj(hT, chunks, evac, hoff=0):
            s = w_take()
            for (t0, n) in chunks:
                b = ACC_BANKS[accst["i"] % len(ACC_BANKS)]
                accst["i"] += 1
                acc = bank(b, n)
                for kc in range(16):
                    S.op("pe", lambda e, acc=acc, kc=kc, t0=t0, n=n, s=s: e.matmul(
                        acc, ring[s][:, kc, :], hT[:, kc, t0 - hoff:t0 - hoff + n], start=(kc == 0), stop=(kc == 15)),
                        reads=[("w", s), "hT"], writes=[("ps", b)], signal=(kc == 15))
                if n >= 256:
                    flush_deferred()
                evac(acc, ("ps", b), t0, n)
                if fillers:
                    fillers.pop(0)()
            w_done()

        def rstd_from(ssq_ap, dst, n):
            S.op("dve", lambda e: e.tensor_scalar(dst, ssq_ap, 1.0 / n, EPS, ALU.mult, ALU.add), reads=["stat"], writes=["stat"])
            S.op("act", lambda e: e.activation(dst, dst, AF.Sqrt), reads=["stat"], writes=["stat"])
            S.op("dve", lambda e: e.reciprocal(dst, dst), reads=["stat"], writes=["stat"])

        tpst = {"i": 0, "banks": [2, 3, 4, 5]}

        def transpose_rows(hb, dstT, tcol, hbk="hb"):
            for q4 in range(4):
                hlf = tpst["i"] % len(tpst["banks"])
                tpst["i"] += 1
                pt = bank(tpst["banks"][hlf]).bitcast(BF16)[:, 0:512]
                for j in range(4):
                    kc = q4 * 4 + j
                    S.op("pe", lambda e, pt=pt, j=j, kc=kc: e.transpose(pt[:, j * 128:(j + 1) * 128], hb[:, kc * 128:(kc + 1) * 128], ident),
                         reads=[hbk, "cbf"], writes=[("ps", tpst["banks"][hlf])], signal=(j == 3))
                dst = dstT[:, q4 * 4:q4 * 4 + 4, tcol:tcol + 128]
                src = pt.rearrange("p (j c) -> p j c", j=4)
                eng = "act" if q4 % 2 == 0 else "dve"
                if eng == "act":
                    S.op("act", lambda e, dst=dst, src=src: e.copy(dst, src), reads=[("ps", tpst["banks"][hlf])], writes=[("hT", tcol, q4)])
                else:
                    S.op("dve", lambda e, dst=dst, src=src: e.tensor_copy(dst, src), reads=[("ps", tpst["banks"][hlf])], writes=[("hT", tcol, q4)])

        def epi_alloc(ntile, wo0=None):
            E = top.sub()
            ctx = {}
            wa = E.bf(16 * 512).rearrange("p (k c) -> p k c", k=16) if wo0 is None else wo0
            wb_ = E.bf(16 * 512).rearrange("p (k c) -> p k c", k=16)
            ctx["wo"] = [wa, wb_]
            ctx["gpost"] = E.f32(2048)
            ctx["xs2"] = [E.f32(2048), E.f32(2048)]
            ctx["ssq"] = E.f32(64)
            ctx["junk2"] = E.bf(512)
            ctx["Ybuf"] = E.f32(ntile * 2048).rearrange("p (j d) -> p j d", j=ntile)
            return ctx

        def epi_prefetch(ctx, layer, chunks, resid_src=None, extra=()):
            pref = ctx.setdefault("pref", set())
            for c in chunks:
                S.dma("pool", ctx["wo"][c % 2].rearrange("p k c -> p (k c)"), woseq[layer * 4 + c], ("wo", c % 2), writes=[("wo", c % 2)] + list(extra))
                pref.add(c)
            if resid_src is not None:
                S.dma("sp", ctx["gpost"], gvec[1 + 2 * layer], "c", writes=["gpost"] + list(extra))
                for j in range(2):
                    S.dma("sp", ctx["xs2"][j], resid_src[j * 128:(j + 1) * 128, :], ("x", j), writes=[("xs2", j)] + list(extra))
                pref.add("res")

        ymixT = top.bf(16 * 1152).rearrange("p (f t) -> p f t", f=16)
        L0 = top.sub()
        hT = L0.bf(16 * 2048).rearrange("p (k t) -> p k t", k=16)
        W0 = L0.sub()

        def prenorm_phase(reg, src, ntile, gidx, dstT, sb_tiles=None):
            PA = reg.sub()
            gB = PA.f32(2048)
            NXB = 4
            xs = [PA.f32(2048) for _ in range(NXB)] if sb_tiles is None else None
            hbs = [PA.bf(2048), PA.bf(2048)]
            junk = PA.bf(2048)
            def load(t):
                if t < ntile and sb_tiles is None:
                    S.dma("sp", xs[t % NXB], src[t * 128:(t + 1) * 128, :], ("x", t % NXB), writes=[("xs", t % NXB)])

            def stage1(t):
                p = t % 2
                xb = xs[t % NXB] if sb_tiles is None else sb_tiles[t]
                hb = hbs[p]
                xr = ("xs", t % NXB) if sb_tiles is None else ("xsb", t)
                sk = ("pst", p)
                ssq_c = stat[:, 16 + 2 * p:17 + 2 * p]
                rs_c = stat[:, 17 + 2 * p:18 + 2 * p]
                S.op("act", lambda e, xb=xb, ssq_c=ssq_c: e.activation(junk, xb, AF.Square, accum_out=ssq_c), reads=[xr], writes=[sk])
                S.op("dve", lambda e, ssq_c=ssq_c, rs_c=rs_c: e.tensor_scalar(rs_c, ssq_c, 1.0 / D, EPS, ALU.mult, ALU.add), reads=[sk], writes=[sk])
                S.op("act", lambda e, rs_c=rs_c: e.activation(rs_c, rs_c, AF.Sqrt), reads=[sk], writes=[sk])
                S.op("dve", lambda e, rs_c=rs_c: e.reciprocal(rs_c, rs_c), reads=[sk], writes=[sk])
                S.op("dve", lambda e, xb=xb, hb=hb, rs_c=rs_c: e.scalar_tensor_tensor(hb, xb, rs_c, gB, ALU.mult, ALU.mult),
                     reads=[xr, sk, "gB"], writes=[("hb", p)])
                load(t + NXB - 1)

            load(0)
            S.dma("sp", gB, gvec[gidx], "c", writes=["gB"])
            for t in range(1, NXB - 1):
                load(t)
            stage1(0)
            for t in range(ntile):
                if t + 1 < ntile:
                    stage1(t + 1)
                transpose_rows(hbs[t % 2], dstT, t * 128, ("hb", t % 2))
            S.barrier()

        prenorm_phase(W0, xw, 16, 0, hT)

        if stop == -1:
            return finish([(ident, 0)])
        if stop == -2:
            return finish([(ident, 0)])
        if stop == -3:
            return finish([(hT[:, 0, 0:128], 0)])
        if stop == -4:
            return finish([(hT[:, 0, 0:1024], 0)])
        if stop == 1:
            return finish([(hT[:, kc, :], kc * 128) for kc in range(8)])
        PB = W0.sub()
        rope = PB.f32(4096)
        ctab = rope[0:32, 0:2048]
        stab = rope[0:32, 2048:4096]
        S.dma("sp", rope[0:32, :], rope_d, "c", writes=["rope"])
        PP = PB.sub()
        A0 = [PP.f32(1168) for _ in range(2)]
        B1 = [PP.f32(1168) for _ in range(2)]
        B2 = [PP.f32(1168) for _ in range(2)]
        pooledT = [PP.bf(1152) for _ in range(2)]
        agT = [PP.bf(1152) for _ in range(2)]
        pw = PP.bf(512).rearrange("p (j d) -> p j d", j=2)
        tmp16 = PP.f32(16)
        for bufs in (A0, B1, B2):
            for j in range(2):
                S.op("dve", lambda e, b=bufs[j]: e.memset(b[:, 0:16], 0.0), writes=[("pad", id(bufs), j)])
        for g in range(4):
            S.dma("pool", pw.rearrange("p j d -> p (j d)"), poolw_d[g], "pw", writes=["pw"])
            nsteps = g + 1
            wwin = float(2 ** (g + 1))
            for j in range(2):
                def evac_a(acc, pr, t0, n, j=j):
                    S.op("act", lambda e: e.copy(A0[j][:, 16 + t0 - 896:16 + t0 - 896 + n], acc), reads=[pr], writes=[("A0", j)])
                inproj(hT, QCH, evac_a)
                cur, curk = A0[j], ("A0", j)
                for si in range(nsteps):
                    sh = 2 ** si
                    nxt, nk = (B1[j], ("B1", j)) if si % 2 == 0 else (B2[j], ("B2", j))
                    S.op("dve", lambda e, cur=cur, nxt=nxt, sh=sh: e.tensor_tensor(nxt[:, 16:1168], cur[:, 16:1168], cur[:, 16 - sh:1168 - sh], ALU.add),
                         reads=[curk], writes=[nk])
                    cur, curk = nxt, nk
                S.op("dve", lambda e, cur=cur, j=j, wwin=wwin: e.scalar_tensor_tensor(pooledT[j], cur[:, 16:1168], 1.0 / wwin, A0[j][:, 16:1168], ALU.mult, ALU.subtract),
                     reads=[curk, ("A0", j)], writes=[("pooledT", j)])
                S.op("dve", lambda e, cur=cur, g=g: e.tensor_tensor(tmp16, cur[:, 144:160], invfix[:, g * 16:(g + 1) * 16], ALU.mult),
                     reads=[curk, "cf32"], writes=["tmp16"])
                S.op("dve", lambda e, j=j: e.tensor_tensor(pooledT[j][:, 128:144], tmp16, A0[j][:, 144:160], ALU.subtract),
                     reads=["tmp16", ("A0", j), ("pooledT", j)], writes=[("pooledT", j)])
            for j in range(2):
                def evac_g(acc, pr, t0, n, j=j):
                    S.op("act", lambda e: e.activation(agT[j][:, t0 - 896:t0 - 896 + n], acc, AF.Silu), reads=[pr], writes=[("agT", j)])
                inproj(hT, QCH, evac_g)
            for m in range(2):
                for (t0, n) in QCH:
                    b = ACC_BANKS[accst["i"] % len(ACC_BANKS)]
                    accst["i"] += 1
                    acc = bank(b, n)
                    for j in range(2):
                        S.op("pe", lambda e, acc=acc, j=j, m=m, t0=t0, n=n: e.matmul(
                            acc, pw[:, j, m * 128:(m + 1) * 128], pooledT[j][:, t0 - 896:t0 - 896 + n], start=(j == 0), stop=(j == 1)),
                            reads=["pw", ("pooledT", j)], writes=[("ps", b)], signal=(j == 1))
                    f = g * 2 + m
                    S.op("dve", lambda e, acc=acc, f=f, m=m, t0=t0, n=n: e.scalar_tensor_tensor(
                        ymixT[:, f, t0 - 896:t0 - 896 + n], acc, pcol[:, f:f + 1], agT[m][:, t0 - 896:t0 - 896 + n], ALU.mult, ALU.mult),
                        reads=[("ps", b), "pcol", ("agT", m)], writes=["ymixT"])
        S.barrier()

        if stop == 2:
            return finish([(ymixT[:, f, :], f * 128) for f in range(8)])
        PH = PB.sub()
        KT = PH.bf(3 * 2048).rearrange("p (g t) -> p g t", g=3)
        QT = PH.bf(3 * 2048).rearrange("p (g t) -> p g t", g=3)
        VTb = PH.bf(2048)
        Vt = PH.bf(3 * 2048).rearrange("p (g s e) -> p g s e", g=3, s=16)
        gateT = PH.bf(1152)
        qb = [PH.bf(512), PH.bf(512)]
        _r1 = PH.f32(512)
        _r2 = PH.f32(512)
        P0 = PH.bf(10 * 256).rearrange("p (n c) -> p n c", n=10)
        P1 = PH.bf(2 * 4 * 256).rearrange("p (b r c) -> p b r c", b=2, r=4)
        P2 = PH.bf(16 * 72).rearrange("p (r c) -> p r c", r=16)
        rden = PH.f32(512)
        obuf = PH.f32(512)
        rt1 = [_r1, rden]
        rt2 = [_r2, obuf]
        rt1k = [("rt1", 0), "rden"]
        rt2k = [("rt2", 0), "obuf"]
        zt = PH.bf(128)
        S.op("dve", lambda e: e.memset(zt, 0.0), writes=["zt"])
        ropest = {"i": 0}

        def dst_view(buf, g, t0, n):
            return buf[:, g, t0:t0 + n]

        def src_view(acc, g, n):
            return acc

        def res_view(ap512, r, rr):
            return ap512.rearrange("p (l r) -> p r l", r=rr)[:, r, :]

        def make_rope_evac(buf, bname, g):
            def evac(acc, pr, t0, n):
                k = ropest["i"] % 2
                ropest["i"] += 1
                dv = dst_view(buf, g, t0, n)
                S.op("act", lambda e: e.copy(dv, src_view(acc, g, n)), reads=[pr], writes=[bname])
                S.op("act", lambda e: e.copy(qb[k][0:32, 0:n], acc[0:32, :]), reads=[pr], writes=[("qb", k)])
                S.op("act", lambda e: e.copy(rt2[k][0:32, 0:n], acc[0:32, :]), reads=[pr], writes=[rt2k[k]])

                def part2():
                    rb = 2 if k == 0 else 5
                    rs = bank(rb, n)[0:32, :]
                    S.op("pe", lambda e: e.matmul(rs, psw, qb[k][0:32, 0:n], start=True, stop=True), reads=[("qb", k), "cbf"], writes=[("ps", rb)])
                    S.op("dve", lambda e: e.tensor_tensor(rt1[k][0:32, 0:n], rs, stab[:, t0:t0 + n], ALU.mult), reads=[("ps", rb), "rope"], writes=[rt1k[k]])
                    S.op("dve", lambda e: e.tensor_tensor(rt2[k][0:32, 0:n], rt2[k][0:32, 0:n], ctab[:, t0:t0 + n], ALU.mult), reads=[rt2k[k], "rope"], writes=[rt2k[k]])
                    dv32 = dst_view(buf[0:32], g, t0, n)
                    S.op("dve", lambda e: e.tensor_tensor(dv32, src_view(rt1[k][0:32, 0:n], g, n), src_view(rt2[k][0:32, 0:n], g, n), ALU.add),
                         reads=[rt1k[k], rt2k[k], bname], writes=[bname])
                deferred.append(part2)
            return evac

        QLO1 = {0: 128, 1: 96, 2: 0, 3: 0}
        SCALE = 128.0 ** -0.5
        scst = {"i": 0}

        SCB = [4, 5, 3, 2]

        def score_group(jobs, width):
            pb = SCB[scst["i"] % len(SCB)]
            scst["i"] += 1
            rk = ("ps", pb)
            scb = bank(pb)
            nmm = sum(len(j[1]) for j in jobs)
            i = 0
            for ji, (lhsT, parts, pflat, pidx, pname, mask, (lo, hi)) in enumerate(jobs):
                for (rhs, c0, ncol) in parts:
                    i += 1
                    S.op("pe", lambda e, rhs=rhs, c0=c0, ncol=ncol, ji=ji, lhsT=lhsT: e.matmul(scb[:, ji * width + c0:ji * width + c0 + ncol], lhsT, rhs, start=True, stop=True),
                         reads=["KT", "QT"], writes=[rk], signal=(i == nmm))
            contiguous = all(jobs[k][6][1] == width and jobs[k + 1][6][0] == 0 and jobs[k + 1][3] == jobs[k][3] + 1 for k in range(len(jobs) - 1))
            if contiguous:
                a = jobs[0][6][0]
                b_ = (len(jobs) - 1) * width + jobs[-1][6][1]
                p0 = jobs[0][3] * width
                pf = jobs[0][2]
                S.op("act", lambda e, a=a, b_=b_, p0=p0, pf=pf: e.activation(pf[:, p0 + a:p0 + b_], scb[:, a:b_], AF.Exp, scale=SCALE),
                     reads=[rk], writes=[j[4] for j in jobs])
            else:
                for ji, (lhsT, parts, pflat, pidx, pname, mask, (lo, hi)) in enumerate(jobs):
                    S.op("act", lambda e, ji=ji, lo=lo, hi=hi, pflat=pflat, pidx=pidx: e.activation(
                        pflat[:, pidx * width + lo:pidx * width + hi], scb[:, ji * width + lo:ji * width + hi], AF.Exp, scale=SCALE),
                        reads=[rk], writes=[pname])
            for ji, (lhsT, parts, pflat, pidx, pname, mask, (lo, hi)) in enumerate(jobs):
                pt_ = pflat[:, pidx * width + lo:pidx * width + hi]
                S.op("dve", lambda e, pt_=pt_, mask=mask, lo=lo, hi=hi: e.tensor_tensor(pt_, pt_, mask[:, lo:hi], ALU.mult), reads=[pname, "cbf"], writes=[pname])

        P0f = P0.rearrange("p n c -> p (n c)")
        P2f = P2.rearrange("p r c -> p (r c)")

        EP0 = epi_alloc(9)
        for h in range(8):
            accst["i"] = 0
            for g in range(3):
                inproj(hT, ACH0 if g == 0 else ACH, make_rope_evac(KT, "KT", g))
            for g in range(3):
                def v_transposes(g=g):
                    for s4 in range(1 if g == 0 else 0, 4):
                        tb = 3 + (tpst["i"] % 2)
                        tpst["i"] += 1
                        pt = bank(tb).bitcast(BF16)[:, 0:512]
                        for j in range(4):
                            s_ = s4 * 4 + j
                            if g == 0:
                                vin = VTb[:, s_ * 128:(s_ + 1) * 128]
                            elif g == 1:
                                vin = res_view(VTb[:, (s_ // 4) * 512:(s_ // 4 + 1) * 512], s_ % 4, 4)
                            else:
                                vin = res_view(VTb, s_, 16)
                            S.op("pe", lambda e, pt=pt, j=j, vin=vin: e.transpose(pt[:, j * 128:(j + 1) * 128], vin, ident),
                                 reads=["VTb", "cbf"], writes=[("ps", tb)], signal=(j == 3))
                        S.op("dve", lambda e, pt=pt, g=g, s4=s4: e.tensor_copy(Vt[:, g, s4 * 4:s4 * 4 + 4, :], pt.rearrange("p (j c) -> p j c", j=4)),
                             reads=[("ps", tb)], writes=["Vt"])

                def evac_v(acc, pr, t0, n, g=g, v_transposes=v_transposes):
                    dv = VTb[:, t0:t0 + n]
                    S.op("act", lambda e: e.copy(dv, src_view(acc, g, n)), reads=[pr], writes=["VTb"])
                    if t0 == 1536:
                        deferred.append(v_transposes)
                inproj(hT, ACH0 if g == 0 else ACH, evac_v)
            for g in range(3):
                inproj(hT, QCH, make_rope_evac(QT, "QT", g))

            def evac_bg(acc, pr, t0, n):
                S.op("act", lambda e: e.activation(gateT[:, t0 - 896:t0 - 896 + n], acc, AF.Silu), reads=[pr], writes=["gateT"])
            inproj(hT, QCH, evac_bg)
            flush_deferred()
            if h == 7 and stop >= 4:
                epi_prefetch(EP0, 0, [0], resid_src=xw[896:2048, :], extra=["hT"])

            jl = []
            for n_ in range(6, 16):
                if n_ >= 7 and n_ + 1 <= 15:
                    parts = [(QT[:, 0, n_ * 128:n_ * 128 + 256], 0, 256)]; lo, hi = 0, 256
                elif n_ >= 7:
                    parts = [(QT[:, 0, n_ * 128:n_ * 128 + 128], 0, 128)]; lo, hi = 0, 128
                else:
                    parts = [(QT[:, 0, (n_ + 1) * 128:(n_ + 1) * 128 + 128], 128, 128)]; lo, hi = 128, 256
                jl.append((KT[:, 0, n_ * 128:(n_ + 1) * 128], parts, P0f, n_ - 6, ("P0", n_), m_bnd if n_ == 7 else m_std, (lo, hi)))
            for k2 in range(0, 10, 2):
                score_group(jl[k2:k2 + 2], 256)
            jl = []
            for r in range(16):
                jl.append((res_view(KT[:, 2, :], r, 16), [(res_view(QT[:, 2, :], r, 16)[:, 56:128], 0, 72)], P2f, r, ("P2", r), m_g2, (0, 72)))
            score_group(jl[0:7], 72)
            score_group(jl[7:14], 72)
            score_group(jl[14:16], 72)

            def g1_scores(b):
                P1f = P1[:, b % 2].rearrange("p r c -> p (r c)")
                jl = []
                for r in range(4):
                    parts = []
                    lo, hi = 256, 0
                    if b >= 1:
                        q0 = QLO1[b]
                        parts.append((res_view(QT[:, 1, b * 512:(b + 1) * 512], r, 4)[:, q0:128], q0, 128 - q0)); lo, hi = q0, 128
                    if b + 1 <= 3:
                        q0 = QLO1[b + 1]
                        parts.append((res_view(QT[:, 1, (b + 1) * 512:(b + 2) * 512], r, 4)[:, q0:128], 128 + q0, 128 - q0))
                        lo, hi = min(lo, 128 + q0), 256
                    jl.append((res_view(KT[:, 1, b * 512:(b + 1) * 512], r, 4), parts, P1f, r, ("P1", b % 2, r),
                               m_bnd if b == 1 else m_std, (lo, hi)))
                score_group(jl[0:2], 256)
                score_group(jl[2:4], 256)

            def pv_banks(B):
                return (6, 7) if B % 2 == 1 else (0, 1)

            def pv_memset(B):
                ob, db = pv_banks(B)
                S.op("dve", lambda e, O=bank(ob): e.memset(O, 0.0), writes=[("ps", ob)])
                S.op("dve", lambda e, Dn=bank(db): e.memset(Dn, 0.0), writes=[("ps", db)])

            def pv_matmuls(B):
                ob, db = pv_banks(B)
                O = bank(ob)
                Dn = bank(db)
                jobs = []
                tiles = [3] if B == 1 else [0, 1, 2, 3]
                for j in tiles:
                    n_ = B * 4 + j
                    cs = (lambda T, j=j: T[:, j * 128:(j + 1) * 128])
                    jobs.append((Vt[:, 0, n_ - 1, :], P0[:, n_ - 1 - 6, 128:256], cs, ("P0", n_ - 1)))
                    jobs.append((Vt[:, 0, n_, :], P0[:, n_ - 6, 0:128], cs, ("P0", n_)))
                q0 = QLO1[B]
                for r in range(4):
                    cs = (lambda T, r=r, q0=q0: T.rearrange("p (l r) -> p r l", r=4)[:, r, q0:128])
                    jobs.append((Vt[:, 1, (B - 1) * 4 + r, :], P1[:, (B - 1) % 2, r, 128 + q0:256], cs, ("P1", (B - 1) % 2, r)))
                    jobs.append((Vt[:, 1, B * 4 + r, :], P1[:, B % 2, r, q0:128], cs, ("P1", B % 2, r)))
                ll0 = 24 if B == 1 else 0
                pc0 = {1: 0, 2: 8, 3: 40}[B]
                ncol = 32 - ll0
                for r in range(16):
                    cs = (lambda T, r=r, ll0=ll0: T.rearrange("p (l r) -> p r l", r=16)[:, r, ll0:32])
                    jobs.append((Vt[:, 2, r, :], P2[:, r, pc0:pc0 + ncol], cs, ("P2", r)))
                for i, (vl, pr_, cs, pk) in enumerate(jobs):
                    last = (i == len(jobs) - 1)
                    S.op("pe", lambda e, vl=vl, pr_=pr_, cs=cs, last=last, O=O: e.matmul(cs(O), vl, pr_, start=False, stop=last, skip_group_check=True),
                         reads=["Vt", pk], writes=[("ps", ob)], signal=False)
                    S.op("pe", lambda e, pr_=pr_, cs=cs, last=last, Dn=Dn: e.matmul(cs(Dn), ones_bf, pr_, start=False, stop=last, skip_group_check=True),
                         reads=["cbf", pk], writes=[("ps", db)], signal=last)

            def pv_evac(B, h=h):
                ob, db = pv_banks(B)
                O = bank(ob)
                Dn = bank(db)
                c0 = 384 if B == 1 else 0
                tq = B * 512 + c0 - 896
                ncl = 512 - c0
                S.op("dve", lambda e: e.reciprocal(rden[:, c0:512], Dn[:, c0:512]), reads=[("ps", db)], writes=["rden"])
                S.op("dve", lambda e: e.tensor_tensor(obuf[:, c0:512], O[:, c0:512], rden[:, c0:512], ALU.mult), reads=[("ps", ob), "rden"], writes=["obuf"])
                S.op("dve", lambda e: e.tensor_tensor(ymixT[:, 8 + h, tq:tq + ncl], obuf[:, c0:512], gateT[:, tq:tq + ncl], ALU.mult),
                     reads=["obuf", "gateT"], writes=["ymixT"])

            pv_memset(1)
            pv_memset(2)
            g1_scores(0)
            g1_scores(1)
            pv_matmuls(1)
            g1_scores(2)
            pv_matmuls(2)
            pv_evac(1)
            pv_memset(3)
            g1_scores(3)
            pv_matmuls(3)
            pv_evac(2)
            pv_evac(3)
        S.barrier(skip=[("wo", 1)])

        if stop == 3:
            return finish([(ymixT[:, 8 + f, :], f * 128) for f in range(8)])
        def epi_run(ctx, layer, ymT, ntile, resid_src, dst, end_skip=()):
            Ybuf, wo, ssq, junk2, gpost, xs2 = ctx["Ybuf"], ctx["wo"], ctx["ssq"], ctx["junk2"], ctx["gpost"], ctx["xs2"]
            pref = ctx.get("pref", set())

            def load_res(j):
                if j < ntile:
                    S.dma("sp", xs2[j % 2], resid_src[j * 128:(j + 1) * 128, :], ("x", j % 2), writes=[("xs2", j % 2)])

            def post_a(j):
                p = j % 2
                S.op("dve", lambda e, j=j, p=p: e.tensor_reduce(stat[:, 24 + 2 * p:25 + 2 * p], ssq[:, j * 4:j * 4 + 4], AX.X, ALU.add), reads=[("ssq", j)], writes=[("pst2", p)])
                S.op("dve", lambda e, p=p: e.tensor_scalar(stat[:, 25 + 2 * p:26 + 2 * p], stat[:, 24 + 2 * p:25 + 2 * p], 1.0 / D, EPS, ALU.mult, ALU.add), reads=[("pst2", p)], writes=[("pst2", p)])

            def post_b(j):
                p = j % 2
                xb = xs2[p]
                xr = ("xs2", p)
                rs_c = stat[:, 25 + 2 * p:26 + 2 * p]
                S.op("act", lambda e, rs_c=rs_c: e.activation(rs_c, rs_c, AF.Sqrt), reads=[("pst2", p)], writes=[("pst2", p)])
                S.op("dve", lambda e, rs_c=rs_c: e.reciprocal(rs_c, rs_c), reads=[("pst2", p)], writes=[("pst2", p)])
                Yj = Ybuf[:, j, :]
                S.op("dve", lambda e, Yj=Yj, xb=xb, rs_c=rs_c: e.scalar_tensor_tensor(Yj, Yj, rs_c, xb, ALU.mult, ALU.add), reads=[("Y", j), ("pst2", p), xr], writes=[("Y", j)])
                load_res(j + 2)
                if dst(j) is not None:
                    S.dma("sp", dst(j), Yj, "o", reads=[("Y", j)])

            if "res" not in pref:
                S.dma("sp", gpost, gvec[1 + 2 * layer], "c", writes=["gpost"])
                load_res(0)
                load_res(1)
            for c in range(4):
                wb = wo[c % 2]
                if c not in pref:
                    S.dma("pool", wb.rearrange("p k c -> p (k c)"), woseq[layer * 4 + c], ("wo", c % 2), writes=[("wo", c % 2)])
                for j in range(ntile):
                    b = accst["i"] % 4
                    accst["i"] += 1
                    acc = bank(b)
                    for kc in range(16):
                        S.op("pe", lambda e, acc=acc, kc=kc, j=j, wb=wb: e.matmul(acc, ymT[:, kc, j * 128:(j + 1) * 128], wb[:, kc, :], start=(kc == 0), stop=(kc == 15)),
                             reads=[("wo", c % 2), "ymT"], writes=[("ps", b)], signal=(kc == 15))
                    S.op("act", lambda e, acc=acc, j=j, c=c: e.activation(junk2, acc, AF.Square, accum_out=ssq[:, j * 4 + c:j * 4 + c + 1]),
                         reads=[("ps", b)], writes=["junk2", ("ssq", j), ("psr", b)])
                    S.op("dve", lambda e, acc=acc, j=j, c=c: e.tensor_tensor(Ybuf[:, j, c * 512:(c + 1) * 512], acc, gpost[:, c * 512:(c + 1) * 512], ALU.mult),
                         reads=[("ps", b), ("psr", b), "gpost"], writes=[("Y", j)])
                    if c == 3:
                        if j >= 1:
                            post_a(j - 1)
                        if j >= 2:
                            post_b(j - 2)
            post_a(ntile - 1)
            post_b(ntile - 2)
            post_b(ntile - 1)
            S.barrier(skip=end_skip)

        if stop == 4:
            epi_run(EP0, 0, ymixT, 9, xw[896:2048, :], lambda j: (out[(j - 1) * 128:j * 128, :] if j >= 1 else None))
            S._emit_waits("sp", [["o", S.dcnt["o"]]])
            S.emit()
            return nc
        epi_run(EP0, 0, ymixT, 9, xw[896:2048, :], lambda j: x1s[j * 128:(j + 1) * 128, :], end_skip=["o"])

        L1r = top.sub()
        h1T = L1r.bf(16 * 1152).rearrange("p (k t) -> p k t", k=16)
        prenorm_phase(L1r, x1s, 9, 2, h1T, sb_tiles=[EP0["Ybuf"][:, j, :] for j in range(9)])
        L1 = L1r.sub()
        wo_end = arena_t[:, ARENA - 4096:ARENA].bitcast(BF16).rearrange("p (k c) -> p k c", k=16)
        EP1 = epi_alloc(8, wo0=wo_end)
        ym1 = ymixT.rearrange("p f t -> p (f t)")[:, 0:16 * 1024].rearrange("p (f t) -> p f t", f=16)
        OWN = [(128, 512), (640, 512)]
        HAL = [(0, 128), (128, 512), (640, 512)]

        SG = L1.sub()
        vbuf = SG.f32(8 * 1024).rearrange("p (n c) -> p n c", n=8)
        vn = SG.bf(8 * 1024).rearrange("p (n c) -> p n c", n=8)
        vt1 = SG.f32(1024)
        vt2 = SG.f32(1024)
        sgc = SG.f32(2560)
        wsf = SG.f32(512)
        wsb = SG.bf(512).rearrange("p (h i) -> p h i", h=4)
        uTs = [SG.f32(1024) for _ in range(3)]
        sgTs = [SG.f32(1024) for _ in range(3)]
        st1 = SG.f32(128)
        st2 = SG.f32(128)
        junk4 = SG.bf(1024)
        S.dma("sp", sgc, sguc, "c", writes=["sgc"])
        S.dma("sp", wsf, wsT_d, "c", writes=["wsf"])
        for hh in range(4):
            S.op("dve", lambda e, hh=hh: e.tensor_tensor(wsb[:, hh, :], wsf[:, hh * 128:(hh + 1) * 128], m_std[:, 0:128], ALU.mult),
                 reads=["wsf", "cbf"], writes=["wsb"])
        for ct in range(8):
            s = w_take()
            for n4 in range(2):
                b = ACC_BANKS[accst["i"] % len(ACC_BANKS)]
                accst["i"] += 1
                acc = bank(b)
                for nn in range(4):
                    n_ = n4 * 4 + nn
                    for kc in range(16):
                        S.op("pe", lambda e, acc=acc, nn=nn, n_=n_, kc=kc, s=s: e.matmul(
                            acc[:, nn * 128:(nn + 1) * 128], h1T[:, kc, 128 + n_ * 128:128 + (n_ + 1) * 128], ring[s][:, kc, :], start=(kc == 0), stop=(kc == 15)),
                            reads=[("w", s), "hT"], writes=[("ps", b)], signal=(kc == 15 and nn == 3))
                S.op("act", lambda e, acc=acc, n4=n4, ct=ct: e.copy(vbuf[:, n4 * 4:n4 * 4 + 4, ct * 128:(ct + 1) * 128], acc.rearrange("p (n c) -> p n c", n=4)),
                     reads=[("ps", b)], writes=["vbuf"])
            w_done()
        def ln_tile(n_):
            vv = vbuf[:, n_, :]
            S.op("act", lambda e, vv=vv: e.activation(junk4, vv, AF.Copy, accum_out=stat[:, 8:9]), reads=["vbuf"], writes=["junk4", "stat"])
            S.op("act", lambda e, vv=vv: e.activation(junk4, vv, AF.Square, accum_out=stat[:, 9:10]), reads=["vbuf"], writes=["junk4", "stat"])
            S.op("dve", lambda e: e.tensor_scalar(stat[:, 10:11], stat[:, 8:9], 1.0 / 1024, None, ALU.mult), reads=["stat"], writes=["stat"])
            S.op("dve", lambda e: e.tensor_tensor(stat[:, 11:12], stat[:, 10:11], stat[:, 10:11], ALU.mult), reads=["stat"], writes=["stat"])
            S.op("dve", lambda e: e.scalar_tensor_tensor(stat[:, 12:13], stat[:, 9:10], 1.0 / 1024, stat[:, 11:12], ALU.mult, ALU.subtract), reads=["stat"], writes=["stat"])
            S.op("dve", lambda e: e.tensor_scalar(stat[:, 13:14], stat[:, 12:13], EPS, None, ALU.add), reads=["stat"], writes=["stat"])
            S.op("act", lambda e: e.activation(stat[:, 13:14], stat[:, 13:14], AF.Sqrt), reads=["stat"], writes=["stat"])
            S.op("dve", lambda e: e.reciprocal(stat[:, 13:14], stat[:, 13:14]), reads=["stat"], writes=["stat"])
            S.op("dve", lambda e, vv=vv: e.tensor_scalar(vt1, vv, stat[:, 10:11], stat[:, 13:14], ALU.subtract, ALU.mult), reads=["vbuf", "stat"], writes=["vt1"])
            S.op("dve", lambda e: e.tensor_tensor(vt2, vt1, sgc[:, 0:1024], ALU.mult), reads=["vt1", "sgc"], writes=["vt2"])
            S.op("dve", lambda e, n_=n_: e.tensor_tensor(vn[:, n_, :], vt2, sgc[:, 1024:2048], ALU.add), reads=["vt2", "sgc"], writes=["vn"])

        for n2 in range(0, 8, 2):
            fillers.append(lambda n2=n2: (ln_tile(n2), ln_tile(n2 + 1)))
        def sgu_project(ct):
            uT, sgT, kk = uTs[ct % 3], sgTs[ct % 3], ct % 3

            def evac_u(acc, pr, t0, n):
                S.op("act", lambda e: e.copy(uT[:, t0 - 128:t0 - 128 + n], acc), reads=[pr], writes=[("uT", kk)])
            inproj(h1T, OWN, evac_u)

            def evac_cg(acc, pr, t0, n):
                S.op("act", lambda e: e.activation(sgT[:, t0 - 128:t0 - 128 + n], acc, AF.Silu), reads=[pr], writes=[("sgT", kk)])
            inproj(h1T, OWN, evac_cg)

        def sgu_spatial(ct):
            uT, sgT, kk = uTs[ct % 3], sgTs[ct % 3], ct % 3
            hh = ct // 2
            for n4 in range(2):
                pb = 4 + n4
                sp_ = bank(pb)
                for nn in range(4):
                    n_ = n4 * 4 + nn
                    S.op("pe", lambda e, sp_=sp_, nn=nn, n_=n_: e.matmul(sp_[:, nn * 128:(nn + 1) * 128], vn[:, n_, ct * 128:(ct + 1) * 128], wsb[:, hh, :], start=True, stop=True),
                         reads=["vn", "wsb"], writes=[("ps", pb)], signal=(nn == 3))
                for nn in range(4):
                    n_ = n4 * 4 + nn
                    k = nn % 2
                    stt, stk_ = (st1, "st1") if k == 0 else (st2, "st2")
                    S.op("dve", lambda e, sp_=sp_, nn=nn, stt=stt: e.tensor_tensor(stt, sp_[:, nn * 128:(nn + 1) * 128], sgc[:, 2048 + hh * 128:2048 + (hh + 1) * 128], ALU.add),
                         reads=[("ps", pb), "sgc"], writes=[stk_])
                    S.op("dve", lambda e, stt=stt, n_=n_: e.tensor_tensor(stt, stt, uT[:, n_ * 128:(n_ + 1) * 128], ALU.mult), reads=[stk_, ("uT", kk)], writes=[stk_])
                    S.op("dve", lambda e, stt=stt, n_=n_: e.tensor_tensor(ym1[:, ct, n_ * 128:(n_ + 1) * 128], stt, sgT[:, n_ * 128:(n_ + 1) * 128], ALU.mult),
                         reads=[stk_, ("sgT", kk)], writes=["ym1"])

        for step in range(8 + 2):
            if step < 8:
                sgu_project(step)
            if step == 1:
                while fillers:
                    fillers.pop(0)()
            if step >= 2:
                sgu_spatial(step - 2)
        S.barrier()

        if stop == 6:
            return finish([(ym1[:, f, :], f * 128) for f in range(8)])
        CV = L1.sub()
        sigs = [CV.f32(1152), CV.f32(1152)]
        dbfs = [CV.bf(1152), CV.bf(1152)]
        DGs = [CV.bf(31 * 128).rearrange("p (k c) -> p k c", k=31) for _ in range(2)]
        convb = CV.f32(8 * 1024).rearrange("p (c t) -> p c t", c=8)
        sqt = [CV.f32(512), CV.f32(512)]
        MU = CV.f32(1024)
        RS = CV.f32(1024)
        sdg = [CV.f32(512), CV.f32(512)]
        ct1 = [CV.f32(512), CV.f32(512)]
        ct2 = [CV.f32(512), CV.f32(512)]
        cvst = {"i": 0}
        for ct in range(8):
            pp = ct % 2
            sig, dbf, DG = sigs[pp], dbfs[pp], DGs[pp]
            for k in range(31):
                S.op("dve", lambda e, DG=DG, k=k, ct=ct: e.tensor_scalar(DG[:, k, :], ident, pcol[:, 32 + ct * 31 + k:32 + ct * 31 + k + 1], None, ALU.mult),
                     reads=["cbf", "pcol"], writes=[("DG", pp)])

            if ct == 4:
                epi_prefetch(EP1, 1, [0])

            def evac_glu(acc, pr, t0, n, sig=sig, pp=pp):
                S.op("act", lambda e: e.activation(sig[:, t0:t0 + n], acc, AF.Sigmoid), reads=[pr], writes=[("sig", pp)])
            inproj(h1T, HAL, evac_glu)

            def conv_mm(ct=ct, pp=pp, dbf=dbf, DG=DG):
                for c2 in range(2):
                    cb = 2 + (cvst["i"] % 2)
                    cvst["i"] += 1
                    cacc = bank(cb)
                    for k in range(31):
                        S.op("pe", lambda e, cacc=cacc, k=k, c2=c2: e.matmul(cacc, DG[:, k, :], dbf[:, 98 + k + c2 * 512:98 + k + c2 * 512 + 512], start=(k == 0), stop=(k == 30)),
                             reads=[("DG", pp), ("dbuf", pp)], writes=[("ps", cb)], signal=(k == 30))
                    S.op("dve", lambda e, cacc=cacc, c2=c2: e.tensor_scalar(convb[:, ct, c2 * 512:(c2 + 1) * 512], cacc, pcol[:, 8 + ct:9 + ct], None, ALU.add),
                         reads=[("ps", cb), "pcol"], writes=[("conv", ct)])

            def evac_val(acc, pr, t0, n, sig=sig, dbf=dbf, pp=pp, conv_mm=conv_mm):
                S.op("dve", lambda e: e.tensor_tensor(dbf[:, t0:t0 + n], acc, sig[:, t0:t0 + n], ALU.mult), reads=[pr, ("sig", pp)], writes=[("dbuf", pp)])
                if t0 == 640:
                    deferred.append(conv_mm)
            inproj(h1T, HAL, evac_val)
        flush_deferred()
        for c2 in range(2):
            mu_ps = bank(4 + c2)
            sq_ps = bank(6 + c2)
            for ct in range(8):
                S.op("pe", lambda e, mu_ps=mu_ps, ct=ct, c2=c2: e.matmul(mu_ps, ones_f, convb[:, ct, c2 * 512:(c2 + 1) * 512], start=(ct == 0), stop=(ct == 7)),
                     reads=[("conv", ct), "cf32"], writes=[("ps", 4 + c2)], signal=(ct == 7))
            for ct in range(8):
                k = ct % 2
                S.op("act", lambda e, ct=ct, c2=c2, k=k: e.activation(sqt[k], convb[:, ct, c2 * 512:(c2 + 1) * 512], AF.Square), reads=[("conv", ct)], writes=[("sqt", k)])
                S.op("pe", lambda e, sq_ps=sq_ps, ct=ct, k=k: e.matmul(sq_ps, ones_f, sqt[k], start=(ct == 0), stop=(ct == 7)),
                     reads=[("sqt", k), "cf32"], writes=[("ps", 6 + c2)], signal=True)
            mu = MU[:, c2 * 512:(c2 + 1) * 512]
            rs_ = RS[:, c2 * 512:(c2 + 1) * 512]
            S.op("dve", lambda e, mu=mu, mu_ps=mu_ps: e.tensor_scalar(mu, mu_ps, 1.0 / 1024, None, ALU.mult), reads=[("ps", 4 + c2)], writes=["MU"])
            S.op("dve", lambda e, rs_=rs_, mu=mu: e.tensor_tensor(rs_, mu, mu, ALU.mult), reads=["MU"], writes=["RS"])
            S.op("dve", lambda e, rs_=rs_, sq_ps=sq_ps: e.scalar_tensor_tensor(rs_, sq_ps, 1.0 / 1024, rs_, ALU.mult, ALU.subtract), reads=[("ps", 6 + c2), "RS"], writes=["RS"])
            S.op("dve", lambda e, rs_=rs_: e.tensor_scalar(rs_, rs_, EPS, None, ALU.add), reads=["RS"], writes=["RS"])
            S.op("act", lambda e, rs_=rs_: e.activation(rs_, rs_, AF.Sqrt), reads=["RS"], writes=["RS"])
            S.op("dve", lambda e, rs_=rs_: e.reciprocal(rs_, rs_), reads=["RS"], writes=["RS"])
        for ct in range(8):
            def evac_dg(acc, pr, t0, n, ct=ct):
                c0 = t0 - 128
                k = (c0 // 512) % 2
                S.op("act", lambda e: e.activation(sdg[k], acc, AF.Silu), reads=[pr], writes=[("sdg", k)])
                S.op("dve", lambda e: e.tensor_tensor(ct1[k], convb[:, ct, c0:c0 + n], MU[:, c0:c0 + n], ALU.subtract), reads=[("conv", ct), "MU"], writes=[("ct1", k)])
                S.op("dve", lambda e: e.tensor_tensor(ct1[k], ct1[k], RS[:, c0:c0 + n], ALU.mult), reads=[("ct1", k), "RS"], writes=[("ct1", k)])
                S.op("act", lambda e: e.activation(ct2[k], ct1[k], AF.Silu, bias=pcol[:, 24 + ct:25 + ct], scale=pcol[:, 16 + ct:17 + ct]),
                     reads=[("ct1", k), "pcol"], writes=[("ct2", k)])
                S.op("dve", lambda e: e.tensor_tensor(ym1[:, 8 + ct, c0:c0 + n], ct2[k], sdg[k], ALU.mult), reads=[("ct2", k), ("sdg", k)], writes=["ym1"])
            inproj(h1T, OWN, evac_dg)
        S.barrier()

        if stop == 7:
            return finish([(ym1[:, 8 + f, :], f * 128) for f in range(8)])
        epi_run(EP1, 1, ym1, 8, x1s[128:1152, :], lambda j: out[j * 128:(j + 1) * 128, :])
        S._emit_waits("sp", [["o", S.dcnt["o"]]])
        S.emit()
    return nc


_NC_CACHE = {}


def _tile_w(w, cols):
    sub = w[:, cols]
    return sub.reshape(16, 128, 128).transpose(1, 0, 2).reshape(128, 2048)


def kernel(x, e_pre_norm, e_w_in, e_pool_w, e_pool_scale, e_w_out, e_post_norm,
           o_pre_norm, o_w_in, o_sgu_norm_g, o_sgu_norm_b, o_sgu_w, o_sgu_b,
           o_conv_w, o_conv_b, o_conv_norm_g, o_conv_norm_b, o_w_out, o_post_norm):
    f = lambda a: np.ascontiguousarray(np.asarray(a, dtype=np.float32))
    x = f(x)
    w0 = f(e_w_in)[0]
    w1 = f(o_w_in)[0]
    tiles = []
    ar = np.arange(128)
    for g in range(4):
        for j in range(2):
            tiles.append(_tile_w(w0, g * 256 + j * 128 + ar))
        for j in range(2):
            tiles.append(_tile_w(w0, 1024 + g * 256 + j * 128 + ar))
    for h in range(8):
        for g in range(3):
            tiles.append(_tile_w(w0, 5120 + g * 1024 + h * 128 + ar))
        for g in range(3):
            tiles.append(_tile_w(w0, 8192 + g * 1024 + h * 128 + ar))
        for g in range(3):
            tiles.append(_tile_w(w0, 2048 + g * 1024 + h * 128 + ar))
        tiles.append(_tile_w(w0, 11264 + h * 128 + ar))
    for ct in range(8):
        tiles.append(_tile_w(w1, 1024 + ct * 128 + ar))
    for ct in range(8):
        tiles.append(_tile_w(w1, 0 + ct * 128 + ar))
        tiles.append(_tile_w(w1, 2048 + ct * 128 + ar))
    for ct in range(8):
        tiles.append(_tile_w(w1, 4096 + ct * 128 + ar))
        tiles.append(_tile_w(w1, 3072 + ct * 128 + ar))
    for ct in range(8):
        tiles.append(_tile_w(w1, 5120 + ct * 128 + ar))
    wseq = np.ascontiguousarray(np.stack(tiles, 0))
    assert wseq.shape == (144, 128, 2048)
    wos = []
    for wo in (f(e_w_out)[0], f(o_w_out)[0]):
        for c in range(4):
            sub = wo[:, c * 512:(c + 1) * 512]
            wos.append(sub.reshape(16, 128, 512).transpose(1, 0, 2).reshape(128, 8192))
    woseq = np.ascontiguousarray(np.stack(wos, 0))
    bc = lambda v: np.broadcast_to(f(v).reshape(1, -1), (128, f(v).size))
    gvec = np.ascontiguousarray(np.stack([bc(e_pre_norm[0]), bc(e_post_norm[0]), bc(o_pre_norm[0]), bc(o_post_norm[0])], 0))
    sguc = np.ascontiguousarray(np.concatenate([bc(o_sgu_norm_g[0]), bc(o_sgu_norm_b[0]), bc(f(o_sgu_b)[0].reshape(-1))], 1))
    wsT = np.ascontiguousarray(f(o_sgu_w)[0].transpose(2, 0, 1).reshape(128, 512))
    colv = lambda v: f(v).reshape(8, 128).T
    cw = f(o_conv_w)[0]
    cwt = cw.reshape(31, 8, 128).transpose(2, 1, 0).reshape(128, 248)
    pcol = np.ascontiguousarray(np.concatenate([colv(e_pool_scale[0]), colv(o_conv_b[0]), colv(o_conv_norm_g[0]), colv(o_conv_norm_b[0]), cwt], 1))
    poolw = np.ascontiguousarray(f(e_pool_w)[0].reshape(4, 2, 128, 256).transpose(0, 2, 1, 3).reshape(4, 128, 512))
    kk = np.arange(128)[:, None]
    qq = np.arange(128)[None, :]
    ident = np.eye(128, dtype=np.float32)
    ones = np.ones((128, 128), np.float32)
    psw = np.zeros((128, 32), np.float32)
    for m in range(32):
        psw[(m + 16) % 32, m] = 1.0
    m_own = (kk <= qq).astype(np.float32)
    m_next = (kk >= qq).astype(np.float32)
    m_std = np.concatenate([m_own, m_next], 1)
    lq = np.arange(56, 128)[None, :]
    half = 16
    inv_freq = np.power(np.float32(500000.0), -np.arange(0, 32, 2, dtype=np.float32) / np.float32(32)).astype(np.float32)
    in_maps = []
    for c in range(NCORES):
        b, hf = c // 2, c % 2
        if hf == 1:
            xwin = x[b]
            pos = np.arange(2048, dtype=np.float32)
            m_bnd = m_std
            m_g2 = (kk <= lq).astype(np.float32)
            invfix = np.concatenate([np.full((16,), 1.0 / w, np.float32) for w in (2, 4, 8, 16)])
        else:
            xwin = np.concatenate([np.zeros((1024, D), np.float32), x[b, :1024]], 0)
            pos = np.maximum(np.arange(2048, dtype=np.float32) - 1024, 0).astype(np.float32)
            m_bnd = np.concatenate([m_own, np.zeros((128, 128), np.float32)], 1)
            m_g2 = ((kk <= lq) & (kk >= 64)).astype(np.float32)
            tt = np.arange(16, dtype=np.float32)
            invfix = np.concatenate([1.0 / np.minimum(tt + 1, w) for w in (2, 4, 8, 16)]).astype(np.float32)
        ang = (pos[:, None] * inv_freq[None, :]).astype(np.float32)
        cs, sn = np.cos(ang).astype(np.float32).T, np.sin(ang).astype(np.float32).T
        rope = np.concatenate([np.concatenate([cs, cs], 0), np.concatenate([-sn, sn], 0)], 1)
        cbf = np.concatenate([ident, ones, psw, m_std, m_bnd, m_g2], 1)
        cf32 = np.concatenate([np.broadcast_to(invfix[None, :], (128, 64)), ones], 1)
        in_maps.append(dict(
            xw=np.ascontiguousarray(xwin), wseq=wseq, woseq=woseq, gvec=gvec, sguc=sguc, wsT=wsT, pcol=pcol,
            poolw=poolw, rope=np.ascontiguousarray(rope.astype(np.float32)), cbf=np.ascontiguousarray(cbf.astype(np.float32)),
            cf32=np.ascontiguousarray(cf32.astype(np.float32))))
    if "nc" not in _NC_CACHE:
        _NC_CACHE["nc"] = build_nc()
    res = run_bass_kernel_spmd(_NC_CACHE["nc"], in_maps, core_ids=list(range(NCORES)))
    outp = np.empty((4, 2048, D), np.float32)
    for c in range(NCORES):
        b, hf = c // 2, c % 2
        outp[b, hf * 1024:(hf + 1) * 1024] = np.asarray(res.results[c]["out"], dtype=np.float32)
    return outp
```

```python
import contextlib
import os
DBGV = os.environ.get('DBGV', '')
import numpy as np
import concourse.bass as bass
import concourse.mybir as mybir
from concourse.bass_utils import run_bass_kernel_spmd

F32 = mybir.dt.float32
BF16 = mybir.dt.bfloat16
AF = mybir.ActivationFunctionType
ALU = mybir.AluOpType
AX = mybir.AxisListType

D = 2048
EPS = 1e-6
NCORES = 8
ARENA = 51200


class Sched:
    ENG = ("pe", "act", "dve", "pool", "sp")

    def __init__(self, nc, stack):
        self.nc = nc
        self.q = {e: [] for e in self.ENG}
        self.sem = {}
        for e in self.ENG:
            self.sem[e] = stack.enter_context(nc.semaphore("s_" + e))
        self.cnt = {e: 0 for e in self.ENG}
        self.dcnt = {}
        self.waited = {e: {} for e in self.ENG}
        self.res = {}
        self.pending = {e: [] for e in self.ENG}
        self.stack = stack

    def dma_sem(self, key):
        if key not in self.sem:
            self.sem[key] = self.stack.enter_context(self.nc.semaphore("d_%d" % len(self.sem)))
            self.dcnt[key] = 0
        return key

    def _deps(self, reads, writes):
        deps = []
        for r in reads:
            st = self.res.get(r)
            if st and st["w"] is not None:
                deps.append(st["w"])
        for w in writes:
            st = self.res.get(w)
            if st:
                if st["w"] is not None:
                    deps.append(st["w"])
                deps.extend(st["r"])
        return deps

    def _emit_waits(self, eng, deps):
        need = {}
        for tok in deps:
            k, v = tok[0], tok[1]
            if k == eng and eng == "pe":
                continue
            assert v is not None, "unresolved token used as dependency"
            if v > need.get(k, 0):
                need[k] = v
        for k, v in need.items():
            if self.waited[eng].get(k, 0) >= v:
                continue
            self.waited[eng][k] = v
            sem = self.sem[k]
            if DBGV:
                print("  [%s] WAIT %s >= %d" % (eng, k, v))
            self.q[eng].append(lambda e, sem=sem, v=v: e.wait_ge(sem, v))

    def _update(self, tok, reads, writes):
        for r in reads:
            st = self.res.setdefault(r, {"w": None, "r": []})
            st["r"].append(tok)
            if len(st["r"]) > 24:
                st["r"] = st["r"][-24:] if False else st["r"]
        for w in writes:
            self.res[w] = {"w": tok, "r": []}

    def op(self, eng, fn, reads=(), writes=(), signal=True):
        deps = self._deps(reads, writes)
        self._emit_waits(eng, deps)
        if DBGV:
            print("  [%s] OP sig=%s -> %s  r=%s w=%s" % (eng, signal, self.cnt[eng] + (1 if signal else 0), list(reads), list(writes)))
        if signal:
            self.cnt[eng] += 1
            v = self.cnt[eng]
            tok = [eng, v]
            for p in self.pending[eng]:
                p[1] = v
            self.pending[eng] = []
            sem = self.sem[eng]
            self.q[eng].append(lambda e, fn=fn, sem=sem: fn(e).then_inc(sem, 1))
        else:
            tok = [eng, None]
            self.pending[eng].append(tok)
            self.q[eng].append(lambda e, fn=fn: fn(e))
        self._update(tok, reads, writes)
        return tok

    def dma(self, eng, out, in_, key, reads=(), writes=()):
        self.dma_sem(key)
        deps = self._deps(reads, writes)
        self._emit_waits(eng, deps)
        self.dcnt[key] += 16
        if DBGV:
            print("  [%s] DMA %s -> %d r=%s w=%s" % (eng, key, self.dcnt[key], list(reads), list(writes)))
        tok = [key, self.dcnt[key]]
        sem = self.sem[key]
        self.q[eng].append(lambda e, out=out, in_=in_, sem=sem: e.dma_start(out=out, in_=in_).then_inc(sem, 16))
        self._update(tok, reads, writes)
        return tok

    def barrier(self, skip=()):
        for e in self.ENG:
            assert not self.pending[e], "unsignaled op pending on %s at barrier" % e
        toks = [[e, self.cnt[e]] for e in self.ENG if self.cnt[e] > 0]
        toks += [[k, v] for k, v in self.dcnt.items() if v > 0 and k not in skip]
        for e in self.ENG:
            self._emit_waits(e, toks)
        self.res = {k: {"w": self.res[k]["w"], "r": []} for k in skip if k in self.res}

    def emit(self):
        with self.nc.Block() as block:
            @block.tensor
            def _(e):
                for f in self.q["pe"]:
                    f(e)

            @block.scalar
            def _(e):
                for f in self.q["act"]:
                    f(e)

            @block.vector
            def _(e):
                for f in self.q["dve"]:
                    f(e)

            @block.gpsimd
            def _(e):
                for f in self.q["pool"]:
                    f(e)

            @block.sync
            def _(e):
                for f in self.q["sp"]:
                    f(e)


class Arena:
    def __init__(self, t, lo, hi):
        self.t, self.lo, self.hi, self.p = t, lo, hi, lo

    def f32(self, n):
        n = (n + 7) // 8 * 8
        assert self.p + n <= self.hi, "arena overflow %d + %d > %d" % (self.p, n, self.hi)
        ap = self.t[:, self.p:self.p + n]
        self.p += n
        return ap

    def bf(self, n):
        return self.f32((n + 1) // 2).bitcast(BF16)[:, 0:n]

    def sub(self):
        return Arena(self.t, self.p, self.hi)


QCH = [(896, 128), (1024, 512), (1536, 512)]
ACH0 = [(768, 256), (1024, 512), (1536, 512)]
ACH = [(0, 512), (512, 512), (1024, 512), (1536, 512)]
NSLOT = 4


def build_nc(stop=99):
    nc = bass.Bass("TRN2", target_bir_lowering=False)
    dt = lambda name, shape, kind="ExternalInput": nc.dram_tensor(name, shape, F32, kind=kind).ap()
    xw = dt("xw", [2048, D])
    wseq = dt("wseq", [144, 128, 2048])
    woseq = dt("woseq", [8, 128, 8192])
    gvec = dt("gvec", [4, 128, 2048])
    sguc = dt("sguc", [128, 2560])
    wsT_d = dt("wsT", [128, 512])
    pcol_d = dt("pcol", [128, 280])
    poolw_d = dt("poolw", [4, 128, 512])
    rope_d = dt("rope", [32, 4096])
    cbf_d = dt("cbf", [128, 872])
    cf32_d = dt("cf32", [128, 192])
    out = dt("out", [1024, D], kind="ExternalOutput")
    x1s = dt("x1s", [1152, D], kind="Internal")

    with contextlib.ExitStack() as stk:
        arena_t = stk.enter_context(nc.sbuf_tensor("arena", [128, ARENA], F32))
        PS = stk.enter_context(nc.psum_tensor("ps", [128, 4096], F32))
        S = Sched(nc, stk)
        top = Arena(arena_t, 0, ARENA)

        def bank(b, n=512, off=0):
            return PS[:, b * 512 + off: b * 512 + off + n]

        cbf = top.bf(872)
        cf32 = top.f32(192)
        pcol = top.f32(280)
        stat = top.f32(64)
        ring = [top.bf(2048).rearrange("p (k c) -> p k c", k=16) for _ in range(NSLOT)]
        ident = cbf[:, 0:128]
        ones_bf = cbf[:, 128:256]
        psw = cbf[0:32, 256:288]
        m_std = cbf[:, 288:544]
        m_bnd = cbf[:, 544:800]
        m_g2 = cbf[:, 800:872]
        invfix = cf32[:, 0:64]
        ones_f = cf32[:, 64:192]
        S.dma("pool", cbf, cbf_d, "cp", writes=["cbf"])
        S.dma("sp", cf32, cf32_d, "c", writes=["cf32"])
        S.dma("sp", pcol, pcol_d, "c", writes=["pcol"])

        def finish(dumps):
            for i, (src_ap, r0) in enumerate(dumps):
                S.dma("pool", out[r0:r0 + 128, 0:src_ap.shape[-1]], src_ap, "o")
            S._emit_waits("pool", [["o", S.dcnt["o"]]])
            print("op counts", S.cnt, S.dcnt)
            S.emit()
            return nc

        if stop == 0:
            return finish([(cbf[:, 0:128], 0), (cbf[:, 288:544], 128)])
        wstate = {"next": 0}

        def w_issue(upto):
            while wstate["next"] < min(upto, 144):
                i = wstate["next"]
                s = i % NSLOT
                S.dma("pool", ring[s].rearrange("p k c -> p (k c)"), wseq[i], ("w", s), writes=[("w", s)])
                wstate["next"] += 1

        wuse = {"i": 0}

        def w_take():
            i = wuse["i"]
            wuse["i"] += 1
            w_issue(i + 1)
            return i % NSLOT

        def w_done():
            w_issue(wuse["i"] + NSLOT - 1)

        accst = {"i": 0}
        ACC_BANKS = [0, 1, 6, 7]

        deferred = []
        fillers = []

        def flush_deferred():
            fs = list(deferred)
            del deferred[:]
            for f in fs:
                f()

        def inproj(hT, chunks, evac, hoff=0):
            s = w_take()
            for (t0, n) in chunks:
                b = ACC_BANKS[accst["i"] % len(ACC_BANKS)]
                accst["i"] += 1
                acc = bank(b, n)
                for kc in range(16):
                    S.op("pe", lambda e, acc=acc, kc=kc, t0=t0, n=n, s=s: e.matmul(
                        acc, ring[s][:, kc, :], hT[:, kc, t0 - hoff:t0 - hoff + n], start=(kc == 0), stop=(kc == 15)),
                        reads=[("w", s), "hT"], writes=[("ps", b)], signal=(kc == 15))
                if n >= 256:
                    flush_deferred()
                evac(acc, ("ps", b), t0, n)
                if fillers:
                    fillers.pop(0)()
            w_done()

        def rstd_from(ssq_ap, dst, n):
            S.op("dve", lambda e: e.tensor_scalar(dst, ssq_ap, 1.0 / n, EPS, ALU.mult, ALU.add), reads=["stat"], writes=["stat"])
            S.op("act", lambda e: e.activation(dst, dst, AF.Sqrt), reads=["stat"], writes=["stat"])
            S.op("dve", lambda e: e.reciprocal(dst, dst), reads=["stat"], writes=["stat"])

        tpst = {"i": 0, "banks": [2, 3, 4, 5]}

        def transpose_rows(hb, dstT, tcol, hbk="hb"):
            for q4 in range(4):
                hlf = tpst["i"] % len(tpst["banks"])
                tpst["i"] += 1
                pt = bank(tpst["banks"][hlf]).bitcast(BF16)[:, 0:512]
                for j in range(4):
                    kc = q4 * 4 + j
                    S.op("pe", lambda e, pt=pt, j=j, kc=kc: e.transpose(pt[:, j * 128:(j + 1) * 128], hb[:, kc * 128:(kc + 1) * 128], ident),
                         reads=[hbk, "cbf"], writes=[("ps", tpst["banks"][hlf])], signal=(j == 3))
                dst = dstT[:, q4 * 4:q4 * 4 + 4, tcol:tcol + 128]
                src = pt.rearrange("p (j c) -> p j c", j=4)
                eng = "act" if q4 % 2 == 0 else "dve"
                if eng == "act":
                    S.op("act", lambda e, dst=dst, src=src: e.copy(dst, src), reads=[("ps", tpst["banks"][hlf])], writes=[("hT", tcol, q4)])
                else:
                    S.op("dve", lambda e, dst=dst, src=src: e.tensor_copy(dst, src), reads=[("ps", tpst["banks"][hlf])], writes=[("hT", tcol, q4)])

        def epi_alloc(ntile, wo0=None):
            E = top.sub()
            ctx = {}
            wa = E.bf(16 * 512).rearrange("p (k c) -> p k c", k=16) if wo0 is None else wo0
            wb_ = E.bf(16 * 512).rearrange("p (k c) -> p k c", k=16)
            ctx["wo"] = [wa, wb_]
            ctx["gpost"] = E.f32(2048)
            ctx["xs2"] = [E.f32(2048), E.f32(2048)]
            ctx["ssq"] = E.f32(64)
            ctx["junk2"] = E.bf(512)
            ctx["Ybuf"] = E.f32(ntile * 2048).rearrange("p (j d) -> p j d", j=ntile)
            return ctx

        def epi_prefetch(ctx, layer, chunks, resid_src=None, extra=()):
            pref = ctx.setdefault("pref", set())
            for c in chunks:
                S.dma("pool", ctx["wo"][c % 2].rearrange("p k c -> p (k c)"), woseq[layer * 4 + c], ("wo", c % 2), writes=[("wo", c % 2)] + list(extra))
                pref.add(c)
            if resid_src is not None:
                S.dma("sp", ctx["gpost"], gvec[1 + 2 * layer], "c", writes=["gpost"] + list(extra))
                for j in range(2):
                    S.dma("sp", ctx["xs2"][j], resid_src[j * 128:(j + 1) * 128, :], ("x", j), writes=[("xs2", j)] + list(extra))
                pref.add("res")

        ymixT = top.bf(16 * 1152).rearrange("p (f t) -> p f t", f=16)
        L0 = top.sub()
        hT = L0.bf(16 * 2048).rearrange("p (k t) -> p k t", k=16)
        W0 = L0.sub()

        def prenorm_phase(reg, src, ntile, gidx, dstT, sb_tiles=None):
            PA = reg.sub()
            gB = PA.f32(2048)
            NXB = 4
            xs = [PA.f32(2048) for _ in range(NXB)] if sb_tiles is None else None
            hbs = [PA.bf(2048), PA.bf(2048)]
            junk = PA.bf(2048)
            def load(t):
                if t < ntile and sb_tiles is None:
                    S.dma("sp", xs[t % NXB], src[t * 128:(t + 1) * 128, :], ("x", t % NXB), writes=[("xs", t % NXB)])

            def stage1(t):
                p = t % 2
                xb = xs[t % NXB] if sb_tiles is None else sb_tiles[t]
                hb = hbs[p]
                xr = ("xs", t % NXB) if sb_tiles is None else ("xsb", t)
                sk = ("pst", p)
                ssq_c = stat[:, 16 + 2 * p:17 + 2 * p]
                rs_c = stat[:, 17 + 2 * p:18 + 2 * p]
                S.op("act", lambda e, xb=xb, ssq_c=ssq_c: e.activation(junk, xb, AF.Square, accum_out=ssq_c), reads=[xr], writes=[sk])
                S.op("dve", lambda e, ssq_c=ssq_c, rs_c=rs_c: e.tensor_scalar(rs_c, ssq_c, 1.0 / D, EPS, ALU.mult, ALU.add), reads=[sk], writes=[sk])
                S.op("act", lambda e, rs_c=rs_c: e.activation(rs_c, rs_c, AF.Sqrt), reads=[sk], writes=[sk])
                S.op("dve", lambda e, rs_c=rs_c: e.reciprocal(rs_c, rs_c), reads=[sk], writes=[sk])
                S.op("dve", lambda e, xb=xb, hb=hb, rs_c=rs_c: e.scalar_tensor_tensor(hb, xb, rs_c, gB, ALU.mult, ALU.mult),
                     reads=[xr, sk, "gB"], writes=[("hb", p)])
                load(t + NXB - 1)

            load(0)
            S.dma("sp", gB, gvec[gidx], "c", writes=["gB"])
            for t in range(1, NXB - 1):
                load(t)
            stage1(0)
            for t in range(ntile):
                if t + 1 < ntile:
                    stage1(t + 1)
                transpose_rows(hbs[t % 2], dstT, t * 128, ("hb", t % 2))
            S.barrier()

        prenorm_phase(W0, xw, 16, 0, hT)

        if stop == -1:
            return finish([(ident, 0)])
        if stop == -2:
            return finish([(ident, 0)])
        if stop == -3:
            return finish([(hT[:, 0, 0:128], 0)])
        if stop == -4:
            return finish([(hT[:, 0, 0:1024], 0)])
        if stop == 1:
            return finish([(hT[:, kc, :], kc * 128) for kc in range(8)])
        PB = W0.sub()
        rope = PB.f32(4096)
        ctab = rope[0:32, 0:2048]
        stab = rope[0:32, 2048:4096]
        S.dma("sp", rope[0:32, :], rope_d, "c", writes=["rope"])
        PP = PB.sub()
        A0 = [PP.f32(1168) for _ in range(2)]
        B1 = [PP.f32(1168) for _ in range(2)]
        B2 = [PP.f32(1168) for _ in range(2)]
        pooledT = [PP.bf(1152) for _ in range(2)]
        agT = [PP.bf(1152) for _ in range(2)]
        pw = PP.bf(512).rearrange("p (j d) -> p j d", j=2)
        tmp16 = PP.f32(16)
        for bname_, bufs in (("A0", A0), ("B1", B1), ("B2", B2)):
            for j in range(2):
                S.op("dve", lambda e, b=bufs[j]: e.memset(b[:, 0:16], 0.0), writes=[(bname_, j)])
        for g in range(4):
            S.dma("pool", pw.rearrange("p j d -> p (j d)"), poolw_d[g], "pw", writes=["pw"])
            nsteps = g + 1
            wwin = float(2 ** (g + 1))
            for j in range(2):
                def evac_a(acc, pr, t0, n, j=j):
                    S.op("act", lambda e: e.copy(A0[j][:, 16 + t0 - 896:16 + t0 - 896 + n], acc), reads=[pr], writes=[("A0", j)])
                inproj(hT, QCH, evac_a)
                cur, curk = A0[j], ("A0", j)
                for si in range(nsteps):
                    sh = 2 ** si
                    nxt, nk = (B1[j], ("B1", j)) if si % 2 == 0 else (B2[j], ("B2", j))
                    S.op("dve", lambda e, cur=cur, nxt=nxt, sh=sh: e.tensor_tensor(nxt[:, 16:1168], cur[:, 16:1168], cur[:, 16 - sh:1168 - sh], ALU.add),
                         reads=[curk], writes=[nk])
                    cur, curk = nxt, nk
                S.op("dve", lambda e, cur=cur, j=j, wwin=wwin: e.scalar_tensor_tensor(pooledT[j], cur[:, 16:1168], 1.0 / wwin, A0[j][:, 16:1168], ALU.mult, ALU.subtract),
                     reads=[curk, ("A0", j)], writes=[("pooledT", j)])
                S.op("dve", lambda e, cur=cur, g=g: e.tensor_tensor(tmp16, cur[:, 144:160], invfix[:, g * 16:(g + 1) * 16], ALU.mult),
                     reads=[curk, "cf32"], writes=["tmp16"])
                S.op("dve", lambda e, j=j: e.tensor_tensor(pooledT[j][:, 128:144], tmp16, A0[j][:, 144:160], ALU.subtract),
                     reads=["tmp16", ("A0", j), ("pooledT", j)], writes=[("pooledT", j)])
            for j in range(2):
                def evac_g(acc, pr, t0, n, j=j):
                    S.op("act", lambda e: e.activation(agT[j][:, t0 - 896:t0 - 896 + n], acc, AF.Silu), reads=[pr], writes=[("agT", j)])
                inproj(hT, QCH, evac_g)
            for m in range(2):
                for (t0, n) in QCH:
                    b = ACC_BANKS[accst["i"] % len(ACC_BANKS)]
                    accst["i"] += 1
                    acc = bank(b, n)
                    for j in range(2):
                        S.op("pe", lambda e, acc=acc, j=j, m=m, t0=t0, n=n: e.matmul(
                            acc, pw[:, j, m * 128:(m + 1) * 128], pooledT[j][:, t0 - 896:t0 - 896 + n], start=(j == 0), stop=(j == 1)),
                            reads=["pw", ("pooledT", j)], writes=[("ps", b)], signal=(j == 1))
                    f = g * 2 + m
                    S.op("dve", lambda e, acc=acc, f=f, m=m, t0=t0, n=n: e.scalar_tensor_tensor(
                        ymixT[:, f, t0 - 896:t0 - 896 + n], acc, pcol[:, f:f + 1], agT[m][:, t0 - 896:t0 - 896 + n], ALU.mult, ALU.mult),
                        reads=[("ps", b), "pcol", ("agT", m)], writes=["ymixT"])
        S.barrier()

        if stop == 2:
            return finish([(ymixT[:, f, :], f * 128) for f in range(8)])
        PH = PB.sub()
        KT = PH.bf(3 * 2048).rearrange("p (g t) -> p g t", g=3)
        QT = PH.bf(3 * 2048).rearrange("p (g t) -> p g t", g=3)
        VTb = PH.bf(2048)
        Vt = PH.bf(3 * 2048).rearrange("p (g s e) -> p g s e", g=3, s=16)
        gateT = PH.bf(1152)
        qb = [PH.bf(512), PH.bf(512)]
        _r1 = PH.f32(512)
        _r2 = PH.f32(512)
        P0 = PH.bf(10 * 256).rearrange("p (n c) -> p n c", n=10)
        P1 = PH.bf(2 * 4 * 256).rearrange("p (b r c) -> p b r c", b=2, r=4)
        P2 = PH.bf(16 * 72).rearrange("p (r c) -> p r c", r=16)
        rden = PH.f32(512)
        obuf = PH.f32(512)
        rt1 = [_r1, rden]
        rt2 = [_r2, obuf]
        rt1k = [("rt1", 0), "rden"]
        rt2k = [("rt2", 0), "obuf"]
        zt = PH.bf(128)
        S.op("dve", lambda e: e.memset(zt, 0.0), writes=["zt"])
        ropest = {"i": 0}

        def dst_view(buf, g, t0, n):
            return buf[:, g, t0:t0 + n]

        def src_view(acc, g, n):
            return acc

        def res_view(ap512, r, rr):
            return ap512.rearrange("p (l r) -> p r l", r=rr)[:, r, :]

        def make_rope_evac(buf, bname, g):
            def evac(acc, pr, t0, n):
                k = ropest["i"] % 2
                ropest["i"] += 1
                dv = dst_view(buf, g, t0, n)
                S.op("act", lambda e: e.copy(dv, src_view(acc, g, n)), reads=[pr], writes=[bname])
                S.op("act", lambda e: e.copy(qb[k][0:32, 0:n], acc[0:32, :]), reads=[pr], writes=[("qb", k)])
                S.op("act", lambda e: e.copy(rt2[k][0:32, 0:n], acc[0:32, :]), reads=[pr], writes=[rt2k[k]])

                def part2():
                    rb = 2 if k == 0 else 5
                    rs = bank(rb, n)[0:32, :]
                    S.op("pe", lambda e: e.matmul(rs, psw, qb[k][0:32, 0:n], start=True, stop=True), reads=[("qb", k), "cbf"], writes=[("ps", rb)])
                    S.op("dve", lambda e: e.tensor_tensor(rt1[k][0:32, 0:n], rs, stab[:, t0:t0 + n], ALU.mult), reads=[("ps", rb), "rope"], writes=[rt1k[k]])
                    S.op("dve", lambda e: e.tensor_tensor(rt2[k][0:32, 0:n], rt2[k][0:32, 0:n], ctab[:, t0:t0 + n], ALU.mult), reads=[rt2k[k], "rope"], writes=[rt2k[k]])
                    dv32 = dst_view(buf[0:32], g, t0, n)
                    S.op("dve", lambda e: e.tensor_tensor(dv32, src_view(rt1[k][0:32, 0:n], g, n), src_view(rt2[k][0:32, 0:n], g, n), ALU.add),
                         reads=[rt1k[k], rt2k[k], bname], writes=[bname])
                deferred.append(part2)
            return evac

        QLO1 = {0: 128, 1: 96, 2: 0, 3: 0}
        SCALE = 128.0 ** -0.5
        scst = {"i": 0}

        SCB = [4, 5, 3, 2]

        def score_group(jobs, width):
            pb = SCB[scst["i"] % len(SCB)]
            scst["i"] += 1
            rk = ("ps", pb)
            scb = bank(pb)
            nmm = sum(len(j[1]) for j in jobs)
            i = 0
            for ji, (lhsT, parts, pflat, pidx, pname, mask, (lo, hi)) in enumerate(jobs):
                for (rhs, c0, ncol) in parts:
                    i += 1
                    S.op("pe", lambda e, rhs=rhs, c0=c0, ncol=ncol, ji=ji, lhsT=lhsT: e.matmul(scb[:, ji * width + c0:ji * width + c0 + ncol], lhsT, rhs, start=True, stop=True),
                         reads=["KT", "QT"], writes=[rk], signal=(i == nmm))
            contiguous = all(jobs[k][6][1] == width and jobs[k + 1][6][0] == 0 and jobs[k + 1][3] == jobs[k][3] + 1 for k in range(len(jobs) - 1))
            if contiguous:
                a = jobs[0][6][0]
                b_ = (len(jobs) - 1) * width + jobs[-1][6][1]
                p0 = jobs[0][3] * width
                pf = jobs[0][2]
                S.op("act", lambda e, a=a, b_=b_, p0=p0, pf=pf: e.activation(pf[:, p0 + a:p0 + b_], scb[:, a:b_], AF.Exp, scale=SCALE),
                     reads=[rk], writes=[j[4] for j in jobs])
            else:
                for ji, (lhsT, parts, pflat, pidx, pname, mask, (lo, hi)) in enumerate(jobs):
                    S.op("act", lambda e, ji=ji, lo=lo, hi=hi, pflat=pflat, pidx=pidx: e.activation(
                        pflat[:, pidx * width + lo:pidx * width + hi], scb[:, ji * width + lo:ji * width + hi], AF.Exp, scale=SCALE),
                        reads=[rk], writes=[pname])
            for ji, (lhsT, parts, pflat, pidx, pname, mask, (lo, hi)) in enumerate(jobs):
                pt_ = pflat[:, pidx * width + lo:pidx * width + hi]
                S.op("dve", lambda e, pt_=pt_, mask=mask, lo=lo, hi=hi: e.tensor_tensor(pt_, pt_, mask[:, lo:hi], ALU.mult), reads=[pname, "cbf"], writes=[pname])

        P0f = P0.rearrange("p n c -> p (n c)")
        P2f = P2.rearrange("p r c -> p (r c)")

        EP0 = epi_alloc(9)
        for h in range(8):
            accst["i"] = 0
            for g in range(3):
                inproj(hT, ACH0 if g == 0 else ACH, make_rope_evac(KT, "KT", g))
            for g in range(3):
                def v_transposes(g=g):
                    for s4 in range(1 if g == 0 else 0, 4):
                        tb = 3 + (tpst["i"] % 2)
                        tpst["i"] += 1
                        pt = bank(tb).bitcast(BF16)[:, 0:512]
                        j0 = 2 if (g == 0 and s4 == 1) else 0
                        for j in range(j0, 4):
                            s_ = s4 * 4 + j
                            if g == 0:
                                vin = VTb[:, s_ * 128:(s_ + 1) * 128]
                            elif g == 1:
                                vin = res_view(VTb[:, (s_ // 4) * 512:(s_ // 4 + 1) * 512], s_ % 4, 4)
                            else:
                                vin = res_view(VTb, s_, 16)
                            S.op("pe", lambda e, pt=pt, j=j, vin=vin: e.transpose(pt[:, j * 128:(j + 1) * 128], vin, ident),
                                 reads=["VTb", "cbf"], writes=[("ps", tb)], signal=(j == 3))
                        S.op("dve", lambda e, pt=pt, g=g, s4=s4, j0=j0: e.tensor_copy(Vt[:, g, s4 * 4 + j0:s4 * 4 + 4, :], pt.rearrange("p (j c) -> p j c", j=4)[:, j0:4, :]),
                             reads=[("ps", tb)], writes=["Vt"])

                def evac_v(acc, pr, t0, n, g=g, v_transposes=v_transposes):
                    dv = VTb[:, t0:t0 + n]
                    S.op("act", lambda e: e.copy(dv, src_view(acc, g, n)), reads=[pr], writes=["VTb"])
                    if t0 == 1536:
                        deferred.append(v_transposes)
                inproj(hT, ACH0 if g == 0 else ACH, evac_v)
            for g in range(3):
                inproj(hT, QCH, make_rope_evac(QT, "QT", g))

            def evac_bg(acc, pr, t0, n):
                S.op("act", lambda e: e.activation(gateT[:, t0 - 896:t0 - 896 + n], acc, AF.Silu), reads=[pr], writes=["gateT"])
            inproj(hT, QCH, evac_bg)
            flush_deferred()
            if h == 7 and stop >= 4:
                epi_prefetch(EP0, 0, [0], resid_src=xw[896:2048, :], extra=["hT"])

            jl = []
            for n_ in range(6, 16):
                if n_ >= 7 and n_ + 1 <= 15:
                    parts = [(QT[:, 0, n_ * 128:n_ * 128 + 256], 0, 256)]; lo, hi = 0, 256
                elif n_ >= 7:
                    parts = [(QT[:, 0, n_ * 128:n_ * 128 + 128], 0, 128)]; lo, hi = 0, 128
                else:
                    parts = [(QT[:, 0, (n_ + 1) * 128:(n_ + 1) * 128 + 128], 128, 128)]; lo, hi = 128, 256
                jl.append((KT[:, 0, n_ * 128:(n_ + 1) * 128], parts, P0f, n_ - 6, ("P0", n_), m_bnd if n_ == 7 else m_std, (lo, hi)))
            for k2 in range(0, 10, 2):
                score_group(jl[k2:k2 + 2], 256)
            jl = []
            for r in range(16):
                jl.append((res_view(KT[:, 2, :], r, 16), [(res_view(QT[:, 2, :], r, 16)[:, 56:128], 0, 72)], P2f, r, ("P2", r), m_g2, (0, 72)))
            score_group(jl[0:7], 72)
            score_group(jl[7:14], 72)
            score_group(jl[14:16], 72)

            def g1_scores(b):
                P1f = P1[:, b % 2].rearrange("p r c -> p (r c)")
                jl = []
                for r in range(4):
                    parts = []
                    lo, hi = 256, 0
                    if b >= 1:
                        q0 = QLO1[b]
                        parts.append((res_view(QT[:, 1, b * 512:(b + 1) * 512], r, 4)[:, q0:128], q0, 128 - q0)); lo, hi = q0, 128
                    if b + 1 <= 3:
                        q0 = QLO1[b + 1]
                        parts.append((res_view(QT[:, 1, (b + 1) * 512:(b + 2) * 512], r, 4)[:, q0:128], 128 + q0, 128 - q0))
                        lo, hi = min(lo, 128 + q0), 256
                    jl.append((res_view(KT[:, 1, b * 512:(b + 1) * 512], r, 4), parts, P1f, r, ("P1", b % 2, r),
                               m_bnd if b == 1 else m_std, (lo, hi)))
                score_group(jl[0:2], 256)
                score_group(jl[2:4], 256)

            def pv_banks(B):
                return (6, 7) if B % 2 == 1 else (0, 1)

            def pv_memset(B):
                ob, db = pv_banks(B)
                S.op("dve", lambda e, O=bank(ob): e.memset(O, 0.0), writes=[("ps", ob)])
                S.op("dve", lambda e, Dn=bank(db): e.memset(Dn, 0.0), writes=[("ps", db)])

            def pv_matmuls(B):
                ob, db = pv_banks(B)
                O = bank(ob)
                Dn = bank(db)
                jobs = []
                tiles = [3] if B == 1 else [0, 1, 2, 3]
                for j in tiles:
                    n_ = B * 4 + j
                    cs = (lambda T, j=j: T[:, j * 128:(j + 1) * 128])
                    jobs.append((Vt[:, 0, n_ - 1, :], P0[:, n_ - 1 - 6, 128:256], cs, ("P0", n_ - 1)))
                    jobs.append((Vt[:, 0, n_, :], P0[:, n_ - 6, 0:128], cs, ("P0", n_)))
                q0 = QLO1[B]
                for r in range(4):
                    cs = (lambda T, r=r, q0=q0: T.rearrange("p (l r) -> p r l", r=4)[:, r, q0:128])
                    jobs.append((Vt[:, 1, (B - 1) * 4 + r, :], P1[:, (B - 1) % 2, r, 128 + q0:256], cs, ("P1", (B - 1) % 2, r)))
                    jobs.append((Vt[:, 1, B * 4 + r, :], P1[:, B % 2, r, q0:128], cs, ("P1", B % 2, r)))
                ll0 = 24 if B == 1 else 0
                pc0 = {1: 0, 2: 8, 3: 40}[B]
                ncol = 32 - ll0
                for r in range(16):
                    cs = (lambda T, r=r, ll0=ll0: T.rearrange("p (l r) -> p r l", r=16)[:, r, ll0:32])
                    jobs.append((Vt[:, 2, r, :], P2[:, r, pc0:pc0 + ncol], cs, ("P2", r)))
                for i, (vl, pr_, cs, pk) in enumerate(jobs):
                    last = (i == len(jobs) - 1)
                    S.op("pe", lambda e, vl=vl, pr_=pr_, cs=cs, last=last, O=O: e.matmul(cs(O), vl, pr_, start=False, stop=last, skip_group_check=True),
                         reads=["Vt", pk], writes=[("ps", ob)], signal=False)
                    S.op("pe", lambda e, pr_=pr_, cs=cs, last=last, Dn=Dn: e.matmul(cs(Dn), ones_bf, pr_, start=False, stop=last, skip_group_check=True),
                         reads=["cbf", pk], writes=[("ps", db)], signal=last)

            def pv_evac(B, h=h):
                ob, db = pv_banks(B)
                O = bank(ob)
                Dn = bank(db)
                c0 = 384 if B == 1 else 0
                tq = B * 512 + c0 - 896
                ncl = 512 - c0
                S.op("dve", lambda e: e.reciprocal(rden[:, c0:512], Dn[:, c0:512]), reads=[("ps", db)], writes=["rden"])
                S.op("dve", lambda e: e.tensor_tensor(obuf[:, c0:512], O[:, c0:512], rden[:, c0:512], ALU.mult), reads=[("ps", ob), "rden"], writes=["obuf"])
                S.op("dve", lambda e: e.tensor_tensor(ymixT[:, 8 + h, tq:tq + ncl], obuf[:, c0:512], gateT[:, tq:tq + ncl], ALU.mult),
                     reads=["obuf", "gateT"], writes=["ymixT"])

            pv_memset(1)
            pv_memset(2)
            g1_scores(0)
            g1_scores(1)
            pv_matmuls(1)
            g1_scores(2)
            pv_matmuls(2)
            pv_evac(1)
            pv_memset(3)
            g1_scores(3)
            pv_matmuls(3)
            pv_evac(2)
            pv_evac(3)
        S.barrier(skip=[("wo", 1)])

        if stop == 3:
            return finish([(ymixT[:, 8 + f, :], f * 128) for f in range(8)])
        def epi_run(ctx, layer, ymT, ntile, resid_src, dst, end_skip=()):
            Ybuf, wo, ssq, junk2, gpost, xs2 = ctx["Ybuf"], ctx["wo"], ctx["ssq"], ctx["junk2"], ctx["gpost"], ctx["xs2"]
            pref = ctx.get("pref", set())

            def load_res(j):
                if j < ntile:
                    S.dma("sp", xs2[j % 2], resid_src[j * 128:(j + 1) * 128, :], ("x", j % 2), writes=[("xs2", j % 2)])

            def post_a(j):
                p = j % 2
                S.op("dve", lambda e, j=j, p=p: e.tensor_reduce(stat[:, 24 + 2 * p:25 + 2 * p], ssq[:, j * 4:j * 4 + 4], AX.X, ALU.add), reads=[("ssq", j)], writes=[("pst2", p)])
                S.op("dve", lambda e, p=p: e.tensor_scalar(stat[:, 25 + 2 * p:26 + 2 * p], stat[:, 24 + 2 * p:25 + 2 * p], 1.0 / D, EPS, ALU.mult, ALU.add), reads=[("pst2", p)], writes=[("pst2", p)])

            def post_b(j):
                p = j % 2
                xb = xs2[p]
                xr = ("xs2", p)
                rs_c = stat[:, 25 + 2 * p:26 + 2 * p]
                S.op("act", lambda e, rs_c=rs_c: e.activation(rs_c, rs_c, AF.Sqrt), reads=[("pst2", p)], writes=[("pst2", p)])
                S.op("dve", lambda e, rs_c=rs_c: e.reciprocal(rs_c, rs_c), reads=[("pst2", p)], writes=[("pst2", p)])
                Yj = Ybuf[:, j, :]
                S.op("dve", lambda e, Yj=Yj, xb=xb, rs_c=rs_c: e.scalar_tensor_tensor(Yj, Yj, rs_c, xb, ALU.mult, ALU.add), reads=[("Y", j), ("pst2", p), xr], writes=[("Y", j)])
                load_res(j + 2)
                if dst(j) is not None:
                    S.dma("sp", dst(j), Yj, "o", reads=[("Y", j)])

            if "res" not in pref:
                S.dma("sp", gpost, gvec[1 + 2 * layer], "c", writes=["gpost"])
                load_res(0)
                load_res(1)
            for c in range(4):
                wb = wo[c % 2]
                if c not in pref:
                    S.dma("pool", wb.rearrange("p k c -> p (k c)"), woseq[layer * 4 + c], ("wo", c % 2), writes=[("wo", c % 2)])
                for j in range(ntile):
                    b = accst["i"] % 4
                    accst["i"] += 1
                    acc = bank(b)
                    for kc in range(16):
                        S.op("pe", lambda e, acc=acc, kc=kc, j=j, wb=wb: e.matmul(acc, ymT[:, kc, j * 128:(j + 1) * 128], wb[:, kc, :], start=(kc == 0), stop=(kc == 15)),
                             reads=[("wo", c % 2), "ymT"], writes=[("ps", b)], signal=(kc == 15))
                    S.op("act", lambda e, acc=acc, j=j, c=c: e.activation(junk2, acc, AF.Square, accum_out=ssq[:, j * 4 + c:j * 4 + c + 1]),
                         reads=[("ps", b)], writes=["junk2", ("ssq", j), ("psr", b)])
                    S.op("dve", lambda e, acc=acc, j=j, c=c: e.tensor_tensor(Ybuf[:, j, c * 512:(c + 1) * 512], acc, gpost[:, c * 512:(c + 1) * 512], ALU.mult),
                         reads=[("ps", b), ("psr", b), "gpost"], writes=[("Y", j)])
                    if c == 3:
                        if j >= 1:
                            post_a(j - 1)
                        if j >= 2:
                            post_b(j - 2)
            post_a(ntile - 1)
            post_b(ntile - 2)
            post_b(ntile - 1)
            S.barrier(skip=end_skip)

        if stop == 4:
            epi_run(EP0, 0, ymixT, 9, xw[896:2048, :], lambda j: (out[(j - 1) * 128:j * 128, :] if j >= 1 else None))
            S._emit_waits("sp", [["o", S.dcnt["o"]]])
            S.emit()
            return nc
        epi_run(EP0, 0, ymixT, 9, xw[896:2048, :], lambda j: x1s[j * 128:(j + 1) * 128, :], end_skip=["o"])

        L1r = top.sub()
        h1T = L1r.bf(16 * 1152).rearrange("p (k t) -> p k t", k=16)
        prenorm_phase(L1r, x1s, 9, 2, h1T, sb_tiles=[EP0["Ybuf"][:, j, :] for j in range(9)])
        L1 = L1r.sub()
        wo_end = arena_t[:, ARENA - 4096:ARENA].bitcast(BF16).rearrange("p (k c) -> p k c", k=16)
        EP1 = epi_alloc(8, wo0=wo_end)
        ym1 = ymixT.rearrange("p f t -> p (f t)")[:, 0:16 * 1024].rearrange("p (f t) -> p f t", f=16)
        OWN = [(128, 512), (640, 512)]
        HAL = [(0, 128), (128, 512), (640, 512)]

        SG = L1.sub()
        vbuf = SG.f32(8 * 1024).rearrange("p (n c) -> p n c", n=8)
        vn = SG.bf(8 * 1024).rearrange("p (n c) -> p n c", n=8)
        vt1 = SG.f32(1024)
        vt2 = SG.f32(1024)
        sgc = SG.f32(2560)
        wsf = SG.f32(512)
        wsb = SG.bf(512).rearrange("p (h i) -> p h i", h=4)
        uTs = [SG.f32(1024) for _ in range(3)]
        sgTs = [SG.f32(1024) for _ in range(3)]
        st1 = SG.f32(128)
        st2 = SG.f32(128)
        junk4 = SG.bf(1024)
        S.dma("sp", sgc, sguc, "c", writes=["sgc"])
        S.dma("sp", wsf, wsT_d, "c", writes=["wsf"])
        for hh in range(4):
            S.op("dve", lambda e, hh=hh: e.tensor_tensor(wsb[:, hh, :], wsf[:, hh * 128:(hh + 1) * 128], m_std[:, 0:128], ALU.mult),
                 reads=["wsf", "cbf"], writes=["wsb"])
        for ct in range(8):
            s = w_take()
            for n4 in range(2):
                b = ACC_BANKS[accst["i"] % len(ACC_BANKS)]
                accst["i"] += 1
                acc = bank(b)
                for nn in range(4):
                    n_ = n4 * 4 + nn
                    for kc in range(16):
                        S.op("pe", lambda e, acc=acc, nn=nn, n_=n_, kc=kc, s=s: e.matmul(
                            acc[:, nn * 128:(nn + 1) * 128], h1T[:, kc, 128 + n_ * 128:128 + (n_ + 1) * 128], ring[s][:, kc, :], start=(kc == 0), stop=(kc == 15)),
                            reads=[("w", s), "hT"], writes=[("ps", b)], signal=(kc == 15 and nn == 3))
                S.op("act", lambda e, acc=acc, n4=n4, ct=ct: e.copy(vbuf[:, n4 * 4:n4 * 4 + 4, ct * 128:(ct + 1) * 128], acc.rearrange("p (n c) -> p n c", n=4)),
                     reads=[("ps", b)], writes=["vbuf"])
            w_done()
        def ln_tile(n_):
            vv = vbuf[:, n_, :]
            S.op("act", lambda e, vv=vv: e.activation(junk4, vv, AF.Copy, accum_out=stat[:, 8:9]), reads=["vbuf"], writes=["junk4", "stat"])
            S.op("act", lambda e, vv=vv: e.activation(junk4, vv, AF.Square, accum_out=stat[:, 9:10]), reads=["vbuf"], writes=["junk4", "stat"])
            S.op("dve", lambda e: e.tensor_scalar(stat[:, 10:11], stat[:, 8:9], 1.0 / 1024, None, ALU.mult), reads=["stat"], writes=["stat"])
            S.op("dve", lambda e: e.tensor_tensor(stat[:, 11:12], stat[:, 10:11], stat[:, 10:11], ALU.mult), reads=["stat"], writes=["stat"])
            S.op("dve", lambda e: e.scalar_tensor_tensor(stat[:, 12:13], stat[:, 9:10], 1.0 / 1024, stat[:, 11:12], ALU.mult, ALU.subtract), reads=["stat"], writes=["stat"])
            S.op("dve", lambda e: e.tensor_scalar(stat[:, 13:14], stat[:, 12:13], EPS, None, ALU.add), reads=["stat"], writes=["stat"])
            S.op("act", lambda e: e.activation(stat[:, 13:14], stat[:, 13:14], AF.Sqrt), reads=["stat"], writes=["stat"])
            S.op("dve", lambda e: e.reciprocal(stat[:, 13:14], stat[:, 13:14]), reads=["stat"], writes=["stat"])
            S.op("dve", lambda e, vv=vv: e.tensor_scalar(vt1, vv, stat[:, 10:11], stat[:, 13:14], ALU.subtract, ALU.mult), reads=["vbuf", "stat"], writes=["vt1"])
            S.op("dve", lambda e: e.tensor_tensor(vt2, vt1, sgc[:, 0:1024], ALU.mult), reads=["vt1", "sgc"], writes=["vt2"])
            S.op("dve", lambda e, n_=n_: e.tensor_tensor(vn[:, n_, :], vt2, sgc[:, 1024:2048], ALU.add), reads=["vt2", "sgc"], writes=["vn"])

        for n2 in range(0, 8, 2):
            fillers.append(lambda n2=n2: (ln_tile(n2), ln_tile(n2 + 1)))
        def sgu_project(ct):
            uT, sgT, kk = uTs[ct % 3], sgTs[ct % 3], ct % 3

            def evac_u(acc, pr, t0, n):
                S.op("act", lambda e: e.copy(uT[:, t0 - 128:t0 - 128 + n], acc), reads=[pr], writes=[("uT", kk)])
            inproj(h1T, OWN, evac_u)

            def evac_cg(acc, pr, t0, n):
                S.op("act", lambda e: e.activation(sgT[:, t0 - 128:t0 - 128 + n], acc, AF.Silu), reads=[pr], writes=[("sgT", kk)])
            inproj(h1T, OWN, evac_cg)

        def sgu_spatial(ct):
            uT, sgT, kk = uTs[ct % 3], sgTs[ct % 3], ct % 3
            hh = ct // 2
            for n4 in range(2):
                pb = 4 + n4
                sp_ = bank(pb)
                for nn in range(4):
                    n_ = n4 * 4 + nn
                    S.op("pe", lambda e, sp_=sp_, nn=nn, n_=n_: e.matmul(sp_[:, nn * 128:(nn + 1) * 128], vn[:, n_, ct * 128:(ct + 1) * 128], wsb[:, hh, :], start=True, stop=True),
                         reads=["vn", "wsb"], writes=[("ps", pb)], signal=(nn == 3))
                for nn in range(4):
                    n_ = n4 * 4 + nn
                    k = nn % 2
                    stt, stk_ = (st1, "st1") if k == 0 else (st2, "st2")
                    S.op("dve", lambda e, sp_=sp_, nn=nn, stt=stt: e.tensor_tensor(stt, sp_[:, nn * 128:(nn + 1) * 128], sgc[:, 2048 + hh * 128:2048 + (hh + 1) * 128], ALU.add),
                         reads=[("ps", pb), "sgc"], writes=[stk_])
                    S.op("dve", lambda e, stt=stt, n_=n_: e.tensor_tensor(stt, stt, uT[:, n_ * 128:(n_ + 1) * 128], ALU.mult), reads=[stk_, ("uT", kk)], writes=[stk_])
                    S.op("dve", lambda e, stt=stt, n_=n_: e.tensor_tensor(ym1[:, ct, n_ * 128:(n_ + 1) * 128], stt, sgT[:, n_ * 128:(n_ + 1) * 128], ALU.mult),
                         reads=[stk_, ("sgT", kk)], writes=["ym1"])

        for step in range(8 + 2):
            if step < 8:
                sgu_project(step)
            if step == 1:
                while fillers:
                    fillers.pop(0)()
            if step >= 2:
                sgu_spatial(step - 2)
        S.barrier()

        if stop == 6:
            return finish([(ym1[:, f, :], f * 128) for f in range(8)])
        CV = L1.sub()
        sigs = [CV.f32(1152), CV.f32(1152)]
        dbfs = [CV.bf(1152), CV.bf(1152)]
        DGs = [CV.bf(31 * 128).rearrange("p (k c) -> p k c", k=31) for _ in range(2)]
        convb = CV.f32(8 * 1024).rearrange("p (c t) -> p c t", c=8)
        sqt = [CV.f32(512), CV.f32(512)]
        MU = CV.f32(1024)
        RS = CV.f32(1024)
        sdg = [CV.f32(512), CV.f32(512)]
        ct1 = [CV.f32(512), CV.f32(512)]
        ct2 = [CV.f32(512), CV.f32(512)]
        cvst = {"i": 0}
        for ct in range(8):
            pp = ct % 2
            sig, dbf, DG = sigs[pp], dbfs[pp], DGs[pp]
            for k in range(31):
                S.op("dve", lambda e, DG=DG, k=k, ct=ct: e.tensor_scalar(DG[:, k, :], ident, pcol[:, 32 + ct * 31 + k:32 + ct * 31 + k + 1], None, ALU.mult),
                     reads=["cbf", "pcol"], writes=[("DG", pp)])

            if ct == 4:
                epi_prefetch(EP1, 1, [0])

            def evac_glu(acc, pr, t0, n, sig=sig, pp=pp):
                S.op("act", lambda e: e.activation(sig[:, t0:t0 + n], acc, AF.Sigmoid), reads=[pr], writes=[("sig", pp)])
            inproj(h1T, HAL, evac_glu)

            def conv_mm(ct=ct, pp=pp, dbf=dbf, DG=DG):
                for c2 in range(2):
                    cb = 2 + (cvst["i"] % 2)
                    cvst["i"] += 1
                    cacc = bank(cb)
                    for k in range(31):
                        S.op("pe", lambda e, cacc=cacc, k=k, c2=c2: e.matmul(cacc, DG[:, k, :], dbf[:, 98 + k + c2 * 512:98 + k + c2 * 512 + 512], start=(k == 0), stop=(k == 30)),
                             reads=[("DG", pp), ("dbuf", pp)], writes=[("ps", cb)], signal=(k == 30))
                    S.op("dve", lambda e, cacc=cacc, c2=c2: e.tensor_scalar(convb[:, ct, c2 * 512:(c2 + 1) * 512], cacc, pcol[:, 8 + ct:9 + ct], None, ALU.add),
                         reads=[("ps", cb), "pcol"], writes=[("conv", ct)])

            def evac_val(acc, pr, t0, n, sig=sig, dbf=dbf, pp=pp, conv_mm=conv_mm):
                S.op("dve", lambda e: e.tensor_tensor(dbf[:, t0:t0 + n], acc, sig[:, t0:t0 + n], ALU.mult), reads=[pr, ("sig", pp)], writes=[("dbuf", pp)])
                if t0 == 640:
                    deferred.append(conv_mm)
            inproj(h1T, HAL, evac_val)
        flush_deferred()
        for c2 in range(2):
            mu_ps = bank(4 + c2)
            sq_ps = bank(6 + c2)
            for ct in range(8):
                S.op("pe", lambda e, mu_ps=mu_ps, ct=ct, c2=c2: e.matmul(mu_ps, ones_f, convb[:, ct, c2 * 512:(c2 + 1) * 512], start=(ct == 0), stop=(ct == 7)),
                     reads=[("conv", ct), "cf32"], writes=[("ps", 4 + c2)], signal=(ct == 7))
            for ct in range(8):
                k = ct % 2
                S.op("act", lambda e, ct=ct, c2=c2, k=k: e.activation(sqt[k], convb[:, ct, c2 * 512:(c2 + 1) * 512], AF.Square), reads=[("conv", ct)], writes=[("sqt", k)])
                S.op("pe", lambda e, sq_ps=sq_ps, ct=ct, k=k: e.matmul(sq_ps, ones_f, sqt[k], start=(ct == 0), stop=(ct == 7)),
                     reads=[("sqt", k), "cf32"], writes=[("ps", 6 + c2)], signal=True)
            mu = MU[:, c2 * 512:(c2 + 1) * 512]
            rs_ = RS[:, c2 * 512:(c2 + 1) * 512]
            S.op("dve", lambda e, mu=mu, mu_ps=mu_ps: e.tensor_scalar(mu, mu_ps, 1.0 / 1024, None, ALU.mult), reads=[("ps", 4 + c2)], writes=["MU"])
            S.op("dve", lambda e, rs_=rs_, mu=mu: e.tensor_tensor(rs_, mu, mu, ALU.mult), reads=["MU"], writes=["RS"])
            S.op("dve", lambda e, rs_=rs_, sq_ps=sq_ps: e.scalar_tensor_tensor(rs_, sq_ps, 1.0 / 1024, rs_, ALU.mult, ALU.subtract), reads=[("ps", 6 + c2), "RS"], writes=["RS"])
            S.op("dve", lambda e, rs_=rs_: e.tensor_scalar(rs_, rs_, EPS, None, ALU.add), reads=["RS"], writes=["RS"])
            S.op("act", lambda e, rs_=rs_: e.activation(rs_, rs_, AF.Sqrt), reads=["RS"], writes=["RS"])
            S.op("dve", lambda e, rs_=rs_: e.reciprocal(rs_, rs_), reads=["RS"], writes=["RS"])
        for ct in range(8):
            def evac_dg(acc, pr, t0, n, ct=ct):
                c0 = t0 - 128
                k = (c0 // 512) % 2
                S.op("act", lambda e: e.activation(sdg[k], acc, AF.Silu), reads=[pr], writes=[("sdg", k)])
                S.op("dve", lambda e: e.tensor_tensor(ct1[k], convb[:, ct, c0:c0 + n], MU[:, c0:c0 + n], ALU.subtract), reads=[("conv", ct), "MU"], writes=[("ct1", k)])
                S.op("dve", lambda e: e.tensor_tensor(ct1[k], ct1[k], RS[:, c0:c0 + n], ALU.mult), reads=[("ct1", k), "RS"], writes=[("ct1", k)])
                S.op("act", lambda e: e.activation(ct2[k], ct1[k], AF.Silu, bias=pcol[:, 24 + ct:25 + ct], scale=pcol[:, 16 + ct:17 + ct]),
                     reads=[("ct1", k), "pcol"], writes=[("ct2", k)])
                S.op("dve", lambda e: e.tensor_tensor(ym1[:, 8 + ct, c0:c0 + n], ct2[k], sdg[k], ALU.mult), reads=[("ct2", k), ("sdg", k)], writes=["ym1"])
            inproj(h1T, OWN, evac_dg)
        S.barrier()

        if stop == 7:
            return finish([(ym1[:, 8 + f, :], f * 128) for f in range(8)])
        epi_run(EP1, 1, ym1, 8, x1s[128:1152, :], lambda j: out[j * 128:(j + 1) * 128, :])
        S._emit_waits("sp", [["o", S.dcnt["o"]]])
        S.emit()
    return nc


_NC_CACHE = {}


def _tile_w(w, cols):
    sub = w[:, cols]
    return sub.reshape(16, 128, 128).transpose(1, 0, 2).reshape(128, 2048)


def kernel(x, e_pre_norm, e_w_in, e_pool_w, e_pool_scale, e_w_out, e_post_norm,
           o_pre_norm, o_w_in, o_sgu_norm_g, o_sgu_norm_b, o_sgu_w, o_sgu_b,
           o_conv_w, o_conv_b, o_conv_norm_g, o_conv_norm_b, o_w_out, o_post_norm):
    f = lambda a: np.ascontiguousarray(np.asarray(a, dtype=np.float32))
    x = f(x)
    w0 = f(e_w_in)[0]
    w1 = f(o_w_in)[0]
    tiles = []
    ar = np.arange(128)
    for g in range(4):
        for j in range(2):
            tiles.append(_tile_w(w0, g * 256 + j * 128 + ar))
        for j in range(2):
            tiles.append(_tile_w(w0, 1024 + g * 256 + j * 128 + ar))
    for h in range(8):
        for g in range(3):
            tiles.append(_tile_w(w0, 5120 + g * 1024 + h * 128 + ar))
        for g in range(3):
            tiles.append(_tile_w(w0, 8192 + g * 1024 + h * 128 + ar))
        for g in range(3):
            tiles.append(_tile_w(w0, 2048 + g * 1024 + h * 128 + ar))
        tiles.append(_tile_w(w0, 11264 + h * 128 + ar))
    for ct in range(8):
        tiles.append(_tile_w(w1, 1024 + ct * 128 + ar))
    for ct in range(8):
        tiles.append(_tile_w(w1, 0 + ct * 128 + ar))
        tiles.append(_tile_w(w1, 2048 + ct * 128 + ar))
    for ct in range(8):
        tiles.append(_tile_w(w1, 4096 + ct * 128 + ar))
        tiles.append(_tile_w(w1, 3072 + ct * 128 + ar))
    for ct in range(8):
        tiles.append(_tile_w(w1, 5120 + ct * 128 + ar))
    wseq = np.ascontiguousarray(np.stack(tiles, 0))
    assert wseq.shape == (144, 128, 2048)
    wos = []
    for wo in (f(e_w_out)[0], f(o_w_out)[0]):
        for c in range(4):
            sub = wo[:, c * 512:(c + 1) * 512]
            wos.append(sub.reshape(16, 128, 512).transpose(1, 0, 2).reshape(128, 8192))
    woseq = np.ascontiguousarray(np.stack(wos, 0))
    bc = lambda v: np.broadcast_to(f(v).reshape(1, -1), (128, f(v).size))
    gvec = np.ascontiguousarray(np.stack([bc(e_pre_norm[0]), bc(e_post_norm[0]), bc(o_pre_norm[0]), bc(o_post_norm[0])], 0))
    sguc = np.ascontiguousarray(np.concatenate([bc(o_sgu_norm_g[0]), bc(o_sgu_norm_b[0]), bc(f(o_sgu_b)[0].reshape(-1))], 1))
    wsT = np.ascontiguousarray(f(o_sgu_w)[0].transpose(2, 0, 1).reshape(128, 512))
    colv = lambda v: f(v).reshape(8, 128).T
    cw = f(o_conv_w)[0]
    cwt = cw.reshape(31, 8, 128).transpose(2, 1, 0).reshape(128, 248)
    pcol = np.ascontiguousarray(np.concatenate([colv(e_pool_scale[0]), colv(o_conv_b[0]), colv(o_conv_norm_g[0]), colv(o_conv_norm_b[0]), cwt], 1))
    poolw = np.ascontiguousarray(f(e_pool_w)[0].reshape(4, 2, 128, 256).transpose(0, 2, 1, 3).reshape(4, 128, 512))
    kk = np.arange(128)[:, None]
    qq = np.arange(128)[None, :]
    ident = np.eye(128, dtype=np.float32)
    ones = np.ones((128, 128), np.float32)
    psw = np.zeros((128, 32), np.float32)
    for m in range(32):
        psw[(m + 16) % 32, m] = 1.0
    m_own = (kk <= qq).astype(np.float32)
    m_next = (kk >= qq).astype(np.float32)
    m_std = np.concatenate([m_own, m_next], 1)
    lq = np.arange(56, 128)[None, :]
    half = 16
    inv_freq = np.power(np.float32(500000.0), -np.arange(0, 32, 2, dtype=np.float32) / np.float32(32)).astype(np.float32)
    in_maps = []
    for c in range(NCORES):
        b, hf = c // 2, c % 2
        if hf == 1:
            xwin = x[b]
            pos = np.arange(2048, dtype=np.float32)
            m_bnd = m_std
            m_g2 = (kk <= lq).astype(np.float32)
            invfix = np.concatenate([np.full((16,), 1.0 / w, np.float32) for w in (2, 4, 8, 16)])
        else:
            xwin = np.concatenate([np.zeros((1024, D), np.float32), x[b, :1024]], 0)
            pos = np.maximum(np.arange(2048, dtype=np.float32) - 1024, 0).astype(np.float32)
            m_bnd = np.concatenate([m_own, np.zeros((128, 128), np.float32)], 1)
            m_g2 = ((kk <= lq) & (kk >= 64)).astype(np.float32)
            tt = np.arange(16, dtype=np.float32)
            invfix = np.concatenate([1.0 / np.minimum(tt + 1, w) for w in (2, 4, 8, 16)]).astype(np.float32)
        ang = (pos[:, None] * inv_freq[None, :]).astype(np.float32)
        cs, sn = np.cos(ang).astype(np.float32).T, np.sin(ang).astype(np.float32).T
        rope = np.concatenate([np.concatenate([cs, cs], 0), np.concatenate([-sn, sn], 0)], 1)
        cbf = np.concatenate([ident, ones, psw, m_std, m_bnd, m_g2], 1)
        cf32 = np.concatenate([np.broadcast_to(invfix[None, :], (128, 64)), ones], 1)
        in_maps.append(dict(
            xw=np.ascontiguousarray(xwin), wseq=wseq, woseq=woseq, gvec=gvec, sguc=sguc, wsT=wsT, pcol=pcol,
            poolw=poolw, rope=np.ascontiguousarray(rope.astype(np.float32)), cbf=np.ascontiguousarray(cbf.astype(np.float32)),
            cf32=np.ascontiguousarray(cf32.astype(np.float32))))
    if "nc" not in _NC_CACHE:
        _NC_CACHE["nc"] = build_nc()
    res = run_bass_kernel_spmd(_NC_CACHE["nc"], in_maps, core_ids=list(range(NCORES)))
    outp = np.empty((4, 2048, D), np.float32)
    for c in range(NCORES):
        b, hf = c // 2, c % 2
        outp[b, hf * 1024:(hf + 1) * 1024] = np.asarray(res.results[c]["out"], dtype=np.float32)
    return outp
```

```python
import contextlib
import os
DBGV = os.environ.get('DBGV', '')
import numpy as np
import concourse.bass as bass
import concourse.mybir as mybir
from concourse.bass_utils import run_bass_kernel_spmd

F32 = mybir.dt.float32
BF16 = mybir.dt.bfloat16
AF = mybir.ActivationFunctionType
ALU = mybir.AluOpType
AX = mybir.AxisListType

D = 2048
EPS = 1e-6
NCORES = 8
ARENA = 51200


class Sched:
    ENG = ("pe", "act", "dve", "pool", "sp")

    def __init__(self, nc, stack):
        self.nc = nc
        self.q = {e: [] for e in self.ENG}
        self.sem = {}
        for e in self.ENG:
            self.sem[e] = stack.enter_context(nc.semaphore("s_" + e))
        self.cnt = {e: 0 for e in self.ENG}
        self.dcnt = {}
        self.waited = {e: {} for e in self.ENG}
        self.res = {}
        self.pending = {e: [] for e in self.ENG}
        self.stack = stack

    def dma_sem(self, key):
        if key not in self.sem:
            self.sem[key] = self.stack.enter_context(self.nc.semaphore("d_%d" % len(self.sem)))
            self.dcnt[key] = 0
        return key

    def _deps(self, reads, writes):
        deps = []
        for r in reads:
            st = self.res.get(r)
            if st and st["w"] is not None:
                deps.append(st["w"])
        for w in writes:
            st = self.res.get(w)
            if st:
                if st["w"] is not None:
                    deps.append(st["w"])
                deps.extend(st["r"])
        return deps

    def _emit_waits(self, eng, deps):
        need = {}
        for tok in deps:
            k, v = tok[0], tok[1]
            if k == eng and eng == "pe":
                continue
            assert v is not None, "unresolved token used as dependency"
            if v > need.get(k, 0):
                need[k] = v
        for k, v in need.items():
            if self.waited[eng].get(k, 0) >= v:
                continue
            self.waited[eng][k] = v
            sem = self.sem[k]
            if DBGV:
                print("  [%s] WAIT %s >= %d" % (eng, k, v))
            self.q[eng].append(lambda e, sem=sem, v=v: e.wait_ge(sem, v))

    def _update(self, tok, reads, writes):
        for r in reads:
            st = self.res.setdefault(r, {"w": None, "r": []})
            st["r"].append(tok)
            if len(st["r"]) > 24:
                st["r"] = st["r"][-24:] if False else st["r"]
        for w in writes:
            self.res[w] = {"w": tok, "r": []}

    def op(self, eng, fn, reads=(), writes=(), signal=True):
        deps = self._deps(reads, writes)
        self._emit_waits(eng, deps)
        if DBGV:
            print("  [%s] OP sig=%s -> %s  r=%s w=%s" % (eng, signal, self.cnt[eng] + (1 if signal else 0), list(reads), list(writes)))
        if signal:
            self.cnt[eng] += 1
            v = self.cnt[eng]
            tok = [eng, v]
            for p in self.pending[eng]:
                p[1] = v
            self.pending[eng] = []
            sem = self.sem[eng]
            self.q[eng].append(lambda e, fn=fn, sem=sem: fn(e).then_inc(sem, 1))
        else:
            tok = [eng, None]
            self.pending[eng].append(tok)
            self.q[eng].append(lambda e, fn=fn: fn(e))
        self._update(tok, reads, writes)
        return tok

    def dma(self, eng, out, in_, key, reads=(), writes=()):
        self.dma_sem(key)
        deps = self._deps(reads, writes)
        self._emit_waits(eng, deps)
        self.dcnt[key] += 16
        if DBGV:
            print("  [%s] DMA %s -> %d r=%s w=%s" % (eng, key, self.dcnt[key], list(reads), list(writes)))
        tok = [key, self.dcnt[key]]
        sem = self.sem[key]
        self.q[eng].append(lambda e, out=out, in_=in_, sem=sem: e.dma_start(out=out, in_=in_).then_inc(sem, 16))
        self._update(tok, reads, writes)
        return tok

    def barrier(self, skip=()):
        for e in self.ENG:
            assert not self.pending[e], "unsignaled op pending on %s at barrier" % e
        toks = [[e, self.cnt[e]] for e in self.ENG if self.cnt[e] > 0]
        toks += [[k, v] for k, v in self.dcnt.items() if v > 0 and k not in skip]
        for e in self.ENG:
            self._emit_waits(e, toks)
        self.res = {k: {"w": self.res[k]["w"], "r": []} for k in skip if k in self.res}

    def emit(self):
        with self.nc.Block() as block:
            @block.tensor
            def _(e):
                for f in self.q["pe"]:
                    f(e)

            @block.scalar
            def _(e):
                for f in self.q["act"]:
                    f(e)

            @block.vector
            def _(e):
                for f in self.q["dve"]:
                    f(e)

            @block.gpsimd
            def _(e):
                for f in self.q["pool"]:
                    f(e)

            @block.sync
            def _(e):
                for f in self.q["sp"]:
                    f(e)


class Arena:
    def __init__(self, t, lo, hi):
        self.t, self.lo, self.hi, self.p = t, lo, hi, lo

    def f32(self, n):
        n = (n + 7) // 8 * 8
        assert self.p + n <= self.hi, "arena overflow %d + %d > %d" % (self.p, n, self.hi)
        ap = self.t[:, self.p:self.p + n]
        self.p += n
        return ap

    def bf(self, n):
        return self.f32((n + 1) // 2).bitcast(BF16)[:, 0:n]

    def sub(self):
        return Arena(self.t, self.p, self.hi)


QCH = [(896, 128), (1024, 512), (1536, 512)]
ACH0 = [(768, 256), (1024, 512), (1536, 512)]
ACH = [(0, 512), (512, 512), (1024, 512), (1536, 512)]
NSLOT = 4


def build_nc(stop=99):
    nc = bass.Bass("TRN2", target_bir_lowering=False)
    dt = lambda name, shape, kind="ExternalInput": nc.dram_tensor(name, shape, F32, kind=kind).ap()
    xw = dt("xw", [2048, D])
    wseq = dt("wseq", [144, 128, 2048])
    woseq = dt("woseq", [8, 128, 8192])
    gvec = dt("gvec", [4, 128, 2048])
    sguc = dt("sguc", [128, 2560])
    wsT_d = dt("wsT", [128, 512])
    pcol_d = dt("pcol", [128, 280])
    poolw_d = dt("poolw", [4, 128, 512])
    rope_d = dt("rope", [32, 4096])
    cbf_d = dt("cbf", [128, 872])
    cf32_d = dt("cf32", [128, 192])
    out = dt("out", [1024, D], kind="ExternalOutput")
    x1s = dt("x1s", [1152, D], kind="Internal")

    with contextlib.ExitStack() as stk:
        arena_t = stk.enter_context(nc.sbuf_tensor("arena", [128, ARENA], F32))
        PS = stk.enter_context(nc.psum_tensor("ps", [128, 4096], F32))
        S = Sched(nc, stk)
        top = Arena(arena_t, 0, ARENA)

        def bank(b, n=512, off=0):
            return PS[:, b * 512 + off: b * 512 + off + n]

        cbf = top.bf(872)
        cf32 = top.f32(192)
        pcol = top.f32(280)
        stat = top.f32(64)
        ring = [top.bf(2048).rearrange("p (k c) -> p k c", k=16) for _ in range(NSLOT)]
        ident = cbf[:, 0:128]
        ones_bf = cbf[:, 128:256]
        psw = cbf[0:32, 256:288]
        m_std = cbf[:, 288:544]
        m_bnd = cbf[:, 544:800]
        m_g2 = cbf[:, 800:872]
        invfix = cf32[:, 0:64]
        ones_f = cf32[:, 64:192]
        S.dma("pool", cbf, cbf_d, "cp", writes=["cbf"])
        S.dma("sp", cf32, cf32_d, "c", writes=["cf32"])
        S.dma("sp", pcol, pcol_d, "c", writes=["pcol"])

        def finish(dumps):
            for i, (src_ap, r0) in enumerate(dumps):
                S.dma("pool", out[r0:r0 + 128, 0:src_ap.shape[-1]], src_ap, "o")
            S._emit_waits("pool", [["o", S.dcnt["o"]]])
            print("op counts", S.cnt, S.dcnt)
            S.emit()
            return nc

        if stop == 0:
            return finish([(cbf[:, 0:128], 0), (cbf[:, 288:544], 128)])
        wstate = {"next": 0}

        def w_issue(upto):
            while wstate["next"] < min(upto, wstate.get("cap", 144)):
                i = wstate["next"]
                s = i % NSLOT
                S.dma("pool", ring[s].rearrange("p k c -> p (k c)"), wseq[i], ("w", s), writes=[("w", s)])
                wstate["next"] += 1

        wuse = {"i": 0}

        def w_take():
            i = wuse["i"]
            wuse["i"] += 1
            w_issue(i + 1)
            return i % NSLOT

        def w_done():
            w_issue(wuse["i"] + NSLOT - 1)

        accst = {"i": 0}
        ACC_BANKS = [0, 1, 6, 7]

        deferred = []
        fillers = []

        def flush_deferred():
            fs = list(deferred)
            del deferred[:]
            for f in fs:
                f()

        def inproj(hT, chunks, evac, hoff=0):
            s = w_take()
            for (t0, n) in chunks:
                b = ACC_BANKS[accst["i"] % len(ACC_BANKS)]
                accst["i"] += 1
                acc = bank(b, n)
                for kc in range(16):
                    S.op("pe", lambda e, acc=acc, kc=kc, t0=t0, n=n, s=s: e.matmul(
                        acc, ring[s][:, kc, :], hT[:, kc, t0 - hoff:t0 - hoff + n], start=(kc == 0), stop=(kc == 15)),
                        reads=[("w", s), "hT"], writes=[("ps", b)], signal=(kc == 15))
                if n >= 256:
                    flush_deferred()
                evac(acc, ("ps", b), t0, n)
                if fillers:
                    fillers.pop(0)()
            w_done()

        def rstd_from(ssq_ap, dst, n):
            S.op("dve", lambda e: e.tensor_scalar(dst, ssq_ap, 1.0 / n, EPS, ALU.mult, ALU.add), reads=["stat"], writes=["stat"])
            S.op("act", lambda e: e.activation(dst, dst, AF.Sqrt), reads=["stat"], writes=["stat"])
            S.op("dve", lambda e: e.reciprocal(dst, dst), reads=["stat"], writes=["stat"])

        tpst = {"i": 0, "banks": [2, 3, 4, 5]}

        def transpose_rows(hb, dstT, tcol, hbk="hb"):
            for q4 in range(4):
                hlf = tpst["i"] % len(tpst["banks"])
                tpst["i"] += 1
                pt = bank(tpst["banks"][hlf]).bitcast(BF16)[:, 0:512]
                for j in range(4):
                    kc = q4 * 4 + j
                    S.op("pe", lambda e, pt=pt, j=j, kc=kc: e.transpose(pt[:, j * 128:(j + 1) * 128], hb[:, kc * 128:(kc + 1) * 128], ident),
                         reads=[hbk, "cbf"], writes=[("ps", tpst["banks"][hlf])], signal=(j == 3))
                dst = dstT[:, q4 * 4:q4 * 4 + 4, tcol:tcol + 128]
                src = pt.rearrange("p (j c) -> p j c", j=4)
                eng = "act" if q4 % 2 == 0 else "dve"
                if eng == "act":
                    S.op("act", lambda e, dst=dst, src=src: e.copy(dst, src), reads=[("ps", tpst["banks"][hlf])], writes=[("hT", tcol, q4)])
                else:
                    S.op("dve", lambda e, dst=dst, src=src: e.tensor_copy(dst, src), reads=[("ps", tpst["banks"][hlf])], writes=[("hT", tcol, q4)])

        def epi_alloc(ntile, wo0=None):
            E = top.sub()
            ctx = {}
            wa = E.bf(16 * 512).rearrange("p (k c) -> p k c", k=16) if wo0 is None else wo0
            wb_ = E.bf(16 * 512).rearrange("p (k c) -> p k c", k=16)
            ctx["wo"] = [wa, wb_]
            ctx["gpost"] = E.f32(2048)
            ctx["xs2"] = [E.f32(2048), E.f32(2048)]
            ctx["ssq"] = E.f32(64)
            ctx["junk2"] = E.bf(512)
            ctx["Ybuf"] = E.f32(ntile * 2048).rearrange("p (j d) -> p j d", j=ntile)
            return ctx

        def epi_prefetch(ctx, layer, chunks, resid_src=None, extra=()):
            pref = ctx.setdefault("pref", set())
            for c in chunks:
                S.dma("pool", ctx["wo"][c % 2].rearrange("p k c -> p (k c)"), woseq[layer * 4 + c], ("wo", c % 2), writes=[("wo", c % 2)] + list(extra))
                pref.add(c)
            if resid_src is not None:
                S.dma("sp", ctx["gpost"], gvec[1 + 2 * layer], "c", writes=["gpost"] + list(extra))
                for j in range(2):
                    S.dma("sp", ctx["xs2"][j], resid_src[j * 128:(j + 1) * 128, :], ("x", j), writes=[("xs2", j)] + list(extra))
                pref.add("res")

        ymixT = top.bf(16 * 1152).rearrange("p (f t) -> p f t", f=16)
        L0 = top.sub()
        hT = L0.bf(16 * 2048).rearrange("p (k t) -> p k t", k=16)
        W0 = L0.sub()

        def prenorm_phase(reg, src, ntile, gidx, dstT, sb_tiles=None, gB_pre=None):
            PA = reg.sub()
            gB = PA.f32(2048) if gB_pre is None else gB_pre
            NXB = 4
            xs = [PA.f32(2048) for _ in range(NXB)] if sb_tiles is None else None
            hbs = [PA.bf(2048), PA.bf(2048)]
            junk = PA.bf(2048)
            def load(t):
                if t < ntile and sb_tiles is None:
                    S.dma("sp", xs[t % NXB], src[t * 128:(t + 1) * 128, :], ("x", t % NXB), writes=[("xs", t % NXB)])

            def stage1(t):
                p = t % 2
                xb = xs[t % NXB] if sb_tiles is None else sb_tiles[t]
                hb = hbs[p]
                xr = ("xs", t % NXB) if sb_tiles is None else ("xsb", t)
                sk = ("pst", p)
                ssq_c = stat[:, 16 + 2 * p:17 + 2 * p]
                rs_c = stat[:, 17 + 2 * p:18 + 2 * p]
                S.op("act", lambda e, xb=xb, ssq_c=ssq_c: e.activation(junk, xb, AF.Square, accum_out=ssq_c), reads=[xr], writes=[sk])
                S.op("dve", lambda e, ssq_c=ssq_c, rs_c=rs_c: e.tensor_scalar(rs_c, ssq_c, 1.0 / D, EPS, ALU.mult, ALU.add), reads=[sk], writes=[sk])
                S.op("act", lambda e, rs_c=rs_c: e.activation(rs_c, rs_c, AF.Sqrt), reads=[sk], writes=[sk])
                S.op("dve", lambda e, rs_c=rs_c: e.reciprocal(rs_c, rs_c), reads=[sk], writes=[sk])
                S.op("dve", lambda e, xb=xb, hb=hb, rs_c=rs_c: e.scalar_tensor_tensor(hb, xb, rs_c, gB, ALU.mult, ALU.mult),
                     reads=[xr, sk, "gB"], writes=[("hb", p)])
                load(t + NXB - 1)

            load(0)
            if gB_pre is None:
                S.dma("sp", gB, gvec[gidx], "c", writes=["gB"])
            for t in range(1, NXB - 1):
                load(t)
            stage1(0)
            for t in range(ntile):
                if t + 1 < ntile:
                    stage1(t + 1)
                transpose_rows(hbs[t % 2], dstT, t * 128, ("hb", t % 2))
            S.barrier()

        wstate["cap"] = 96
        w_issue(NSLOT)
        prenorm_phase(W0, xw, 16, 0, hT)

        if stop == -1:
            return finish([(ident, 0)])
        if stop == -2:
            return finish([(ident, 0)])
        if stop == -3:
            return finish([(hT[:, 0, 0:128], 0)])
        if stop == -4:
            return finish([(hT[:, 0, 0:1024], 0)])
        if stop == 1:
            return finish([(hT[:, kc, :], kc * 128) for kc in range(8)])
        PB = W0.sub()
        rope = PB.f32(4096)
        ctab = rope[0:32, 0:2048]
        stab = rope[0:32, 2048:4096]
        S.dma("sp", rope[0:32, :], rope_d, "c", writes=["rope"])
        PP = PB.sub()
        A0 = [PP.f32(1168) for _ in range(2)]
        B1 = [PP.f32(1168) for _ in range(2)]
        B2 = [PP.f32(1168) for _ in range(2)]
        pooledT = [PP.bf(1152) for _ in range(2)]
        agT = [PP.bf(1152) for _ in range(2)]
        pw = PP.bf(512).rearrange("p (j d) -> p j d", j=2)
        tmp16 = PP.f32(16)
        for bname_, bufs in (("A0", A0), ("B1", B1), ("B2", B2)):
            for j in range(2):
                S.op("dve", lambda e, b=bufs[j]: e.memset(b[:, 0:16], 0.0), writes=[(bname_, j)])
        for g in range(4):
            S.dma("pool", pw.rearrange("p j d -> p (j d)"), poolw_d[g], "pw", writes=["pw"])
            nsteps = g + 1
            wwin = float(2 ** (g + 1))
            for j in range(2):
                def evac_a(acc, pr, t0, n, j=j):
                    S.op("act", lambda e: e.copy(A0[j][:, 16 + t0 - 896:16 + t0 - 896 + n], acc), reads=[pr], writes=[("A0", j)])
                inproj(hT, QCH, evac_a)
                cur, curk = A0[j], ("A0", j)
                for si in range(nsteps):
                    sh = 2 ** si
                    nxt, nk = (B1[j], ("B1", j)) if si % 2 == 0 else (B2[j], ("B2", j))
                    S.op("dve", lambda e, cur=cur, nxt=nxt, sh=sh: e.tensor_tensor(nxt[:, 16:1168], cur[:, 16:1168], cur[:, 16 - sh:1168 - sh], ALU.add),
                         reads=[curk], writes=[nk])
                    cur, curk = nxt, nk
                S.op("dve", lambda e, cur=cur, j=j, wwin=wwin: e.scalar_tensor_tensor(pooledT[j], cur[:, 16:1168], 1.0 / wwin, A0[j][:, 16:1168], ALU.mult, ALU.subtract),
                     reads=[curk, ("A0", j)], writes=[("pooledT", j)])
                S.op("dve", lambda e, cur=cur, g=g: e.tensor_tensor(tmp16, cur[:, 144:160], invfix[:, g * 16:(g + 1) * 16], ALU.mult),
                     reads=[curk, "cf32"], writes=["tmp16"])
                S.op("dve", lambda e, j=j: e.tensor_tensor(pooledT[j][:, 128:144], tmp16, A0[j][:, 144:160], ALU.subtract),
                     reads=["tmp16", ("A0", j), ("pooledT", j)], writes=[("pooledT", j)])
            for j in range(2):
                def evac_g(acc, pr, t0, n, j=j):
                    S.op("act", lambda e: e.activation(agT[j][:, t0 - 896:t0 - 896 + n], acc, AF.Silu), reads=[pr], writes=[("agT", j)])
                inproj(hT, QCH, evac_g)
            for m in range(2):
                for (t0, n) in QCH:
                    b = ACC_BANKS[accst["i"] % len(ACC_BANKS)]
                    accst["i"] += 1
                    acc = bank(b, n)
                    for j in range(2):
                        S.op("pe", lambda e, acc=acc, j=j, m=m, t0=t0, n=n: e.matmul(
                            acc, pw[:, j, m * 128:(m + 1) * 128], pooledT[j][:, t0 - 896:t0 - 896 + n], start=(j == 0), stop=(j == 1)),
                            reads=["pw", ("pooledT", j)], writes=[("ps", b)], signal=(j == 1))
                    f = g * 2 + m
                    S.op("dve", lambda e, acc=acc, f=f, m=m, t0=t0, n=n: e.scalar_tensor_tensor(
                        ymixT[:, f, t0 - 896:t0 - 896 + n], acc, pcol[:, f:f + 1], agT[m][:, t0 - 896:t0 - 896 + n], ALU.mult, ALU.mult),
                        reads=[("ps", b), "pcol", ("agT", m)], writes=["ymixT"])
        S.barrier()

        if stop == 2:
            return finish([(ymixT[:, f, :], f * 128) for f in range(8)])
        PH = PB.sub()
        KT = PH.bf(3 * 2048).rearrange("p (g t) -> p g t", g=3)
        QT = PH.bf(3 * 2048).rearrange("p (g t) -> p g t", g=3)
        VTb = PH.bf(2048)
        Vt = PH.bf(3 * 2048).rearrange("p (g s e) -> p g s e", g=3, s=16)
        gateT = PH.bf(1152)
        qb = [PH.bf(512), PH.bf(512)]
        _r1 = PH.f32(512)
        _r2 = PH.f32(512)
        P0 = PH.bf(10 * 256).rearrange("p (n c) -> p n c", n=10)
        P1 = PH.bf(2 * 4 * 256).rearrange("p (b r c) -> p b r c", b=2, r=4)
        P2 = PH.bf(16 * 72).rearrange("p (r c) -> p r c", r=16)
        rden = PH.f32(512)
        obuf = PH.f32(512)
        rt1 = [_r1, rden]
        rt2 = [_r2, obuf]
        rt1k = [("rt1", 0), "rden"]
        rt2k = [("rt2", 0), "obuf"]
        zt = PH.bf(128)
        S.op("dve", lambda e: e.memset(zt, 0.0), writes=["zt"])
        ropest = {"i": 0}

        def dst_view(buf, g, t0, n):
            return buf[:, g, t0:t0 + n]

        def src_view(acc, g, n):
            return acc

        def res_view(ap512, r, rr):
            return ap512.rearrange("p (l r) -> p r l", r=rr)[:, r, :]

        def make_rope_evac(buf, bname, g):
            def evac(acc, pr, t0, n):
                k = ropest["i"] % 2
                ropest["i"] += 1
                dv = dst_view(buf, g, t0, n)
                S.op("act", lambda e: e.copy(dv, src_view(acc, g, n)), reads=[pr], writes=[bname])
                S.op("act", lambda e: e.copy(qb[k][0:32, 0:n], acc[0:32, :]), reads=[pr], writes=[("qb", k)])
                S.op("act", lambda e: e.copy(rt2[k][0:32, 0:n], acc[0:32, :]), reads=[pr], writes=[rt2k[k]])

                def part2():
                    rb = 2 if k == 0 else 5
                    rs = bank(rb, n)[0:32, :]
                    S.op("pe", lambda e: e.matmul(rs, psw, qb[k][0:32, 0:n], start=True, stop=True), reads=[("qb", k), "cbf"], writes=[("ps", rb)])
                    S.op("dve", lambda e: e.tensor_tensor(rt1[k][0:32, 0:n], rs, stab[:, t0:t0 + n], ALU.mult), reads=[("ps", rb), "rope"], writes=[rt1k[k]])
                    S.op("dve", lambda e: e.tensor_tensor(rt2[k][0:32, 0:n], rt2[k][0:32, 0:n], ctab[:, t0:t0 + n], ALU.mult), reads=[rt2k[k], "rope"], writes=[rt2k[k]])
                    dv32 = dst_view(buf[0:32], g, t0, n)
                    S.op("dve", lambda e: e.tensor_tensor(dv32, src_view(rt1[k][0:32, 0:n], g, n), src_view(rt2[k][0:32, 0:n], g, n), ALU.add),
                         reads=[rt1k[k], rt2k[k], bname], writes=[bname])
                deferred.append(part2)
            return evac

        QLO1 = {0: 128, 1: 96, 2: 0, 3: 0}
        SCALE = 128.0 ** -0.5
        scst = {"i": 0}

        SCB = [4, 5, 3, 2]

        def score_group(jobs, width):
            pb = SCB[scst["i"] % len(SCB)]
            scst["i"] += 1
            rk = ("ps", pb)
            scb = bank(pb)
            nmm = sum(len(j[1]) for j in jobs)
            i = 0
            for ji, (lhsT, parts, pflat, pidx, pname, mask, (lo, hi)) in enumerate(jobs):
                for (rhs, c0, ncol) in parts:
                    i += 1
                    S.op("pe", lambda e, rhs=rhs, c0=c0, ncol=ncol, ji=ji, lhsT=lhsT: e.matmul(scb[:, ji * width + c0:ji * width + c0 + ncol], lhsT, rhs, start=True, stop=True),
                         reads=["KT", "QT"], writes=[rk], signal=(i == nmm))
            contiguous = all(jobs[k][6][1] == width and jobs[k + 1][6][0] == 0 and jobs[k + 1][3] == jobs[k][3] + 1 for k in range(len(jobs) - 1))
            if contiguous:
                a = jobs[0][6][0]
                b_ = (len(jobs) - 1) * width + jobs[-1][6][1]
                p0 = jobs[0][3] * width
                pf = jobs[0][2]
                S.op("act", lambda e, a=a, b_=b_, p0=p0, pf=pf: e.activation(pf[:, p0 + a:p0 + b_], scb[:, a:b_], AF.Exp, scale=SCALE),
                     reads=[rk], writes=[j[4] for j in jobs])
            else:
                for ji, (lhsT, parts, pflat, pidx, pname, mask, (lo, hi)) in enumerate(jobs):
                    S.op("act", lambda e, ji=ji, lo=lo, hi=hi, pflat=pflat, pidx=pidx: e.activation(
                        pflat[:, pidx * width + lo:pidx * width + hi], scb[:, ji * width + lo:ji * width + hi], AF.Exp, scale=SCALE),
                        reads=[rk], writes=[pname])
            for ji, (lhsT, parts, pflat, pidx, pname, mask, (lo, hi)) in enumerate(jobs):
                pt_ = pflat[:, pidx * width + lo:pidx * width + hi]
                S.op("dve", lambda e, pt_=pt_, mask=mask, lo=lo, hi=hi: e.tensor_tensor(pt_, pt_, mask[:, lo:hi], ALU.mult), reads=[pname, "cbf"], writes=[pname])

        P0f = P0.rearrange("p n c -> p (n c)")
        P2f = P2.rearrange("p r c -> p (r c)")

        EP0 = epi_alloc(9)
        for h in range(8):
            accst["i"] = 0
            for g in range(3):
                inproj(hT, ACH0 if g == 0 else ACH, make_rope_evac(KT, "KT", g))
            for g in range(3):
                def v_transposes(g=g):
                    for s4 in range(1 if g == 0 else 0, 4):
                        tb = 3 + (tpst["i"] % 2)
                        tpst["i"] += 1
                        pt = bank(tb).bitcast(BF16)[:, 0:512]
                        j0 = 2 if (g == 0 and s4 == 1) else 0
                        for j in range(j0, 4):
                            s_ = s4 * 4 + j
                            if g == 0:
                                vin = VTb[:, s_ * 128:(s_ + 1) * 128]
                            elif g == 1:
                                vin = res_view(VTb[:, (s_ // 4) * 512:(s_ // 4 + 1) * 512], s_ % 4, 4)
                            else:
                                vin = res_view(VTb, s_, 16)
                            S.op("pe", lambda e, pt=pt, j=j, vin=vin: e.transpose(pt[:, j * 128:(j + 1) * 128], vin, ident),
                                 reads=["VTb", "cbf"], writes=[("ps", tb)], signal=(j == 3))
                        S.op("dve", lambda e, pt=pt, g=g, s4=s4, j0=j0: e.tensor_copy(Vt[:, g, s4 * 4 + j0:s4 * 4 + 4, :], pt.rearrange("p (j c) -> p j c", j=4)[:, j0:4, :]),
                             reads=[("ps", tb)], writes=["Vt"])

                def evac_v(acc, pr, t0, n, g=g, v_transposes=v_transposes):
                    dv = VTb[:, t0:t0 + n]
                    S.op("act", lambda e: e.copy(dv, src_view(acc, g, n)), reads=[pr], writes=["VTb"])
                    if t0 == 1536:
                        deferred.append(v_transposes)
                inproj(hT, ACH0 if g == 0 else ACH, evac_v)
            for g in range(3):
                inproj(hT, QCH, make_rope_evac(QT, "QT", g))

            def evac_bg(acc, pr, t0, n):
                S.op("act", lambda e: e.activation(gateT[:, t0 - 896:t0 - 896 + n], acc, AF.Silu), reads=[pr], writes=["gateT"])
            inproj(hT, QCH, evac_bg)
            flush_deferred()
            if h == 7 and stop >= 4:
                epi_prefetch(EP0, 0, [0], resid_src=xw[896:2048, :], extra=["hT"])

            jl = []
            for n_ in range(6, 16):
                if n_ >= 7 and n_ + 1 <= 15:
                    parts = [(QT[:, 0, n_ * 128:n_ * 128 + 256], 0, 256)]; lo, hi = 0, 256
                elif n_ >= 7:
                    parts = [(QT[:, 0, n_ * 128:n_ * 128 + 128], 0, 128)]; lo, hi = 0, 128
                else:
                    parts = [(QT[:, 0, (n_ + 1) * 128:(n_ + 1) * 128 + 128], 128, 128)]; lo, hi = 128, 256
                jl.append((KT[:, 0, n_ * 128:(n_ + 1) * 128], parts, P0f, n_ - 6, ("P0", n_), m_bnd if n_ == 7 else m_std, (lo, hi)))
            for k2 in range(0, 10, 2):
                score_group(jl[k2:k2 + 2], 256)
            jl = []
            for r in range(16):
                jl.append((res_view(KT[:, 2, :], r, 16), [(res_view(QT[:, 2, :], r, 16)[:, 56:128], 0, 72)], P2f, r, ("P2", r), m_g2, (0, 72)))
            score_group(jl[0:7], 72)
            score_group(jl[7:14], 72)
            score_group(jl[14:16], 72)

            def g1_scores(b):
                P1f = P1[:, b % 2].rearrange("p r c -> p (r c)")
                jl = []
                for r in range(4):
                    parts = []
                    lo, hi = 256, 0
                    if b >= 1:
                        q0 = QLO1[b]
                        parts.append((res_view(QT[:, 1, b * 512:(b + 1) * 512], r, 4)[:, q0:128], q0, 128 - q0)); lo, hi = q0, 128
                    if b + 1 <= 3:
                        q0 = QLO1[b + 1]
                        parts.append((res_view(QT[:, 1, (b + 1) * 512:(b + 2) * 512], r, 4)[:, q0:128], 128 + q0, 128 - q0))
                        lo, hi = min(lo, 128 + q0), 256
                    jl.append((res_view(KT[:, 1, b * 512:(b + 1) * 512], r, 4), parts, P1f, r, ("P1", b % 2, r),
                               m_bnd if b == 1 else m_std, (lo, hi)))
                score_group(jl[0:2], 256)
                score_group(jl[2:4], 256)

            def pv_banks(B):
                return (6, 7) if B % 2 == 1 else (0, 1)

            def pv_memset(B):
                ob, db = pv_banks(B)
                S.op("dve", lambda e, O=bank(ob): e.memset(O, 0.0), writes=[("ps", ob)])
                S.op("dve", lambda e, Dn=bank(db): e.memset(Dn, 0.0), writes=[("ps", db)])

            def pv_matmuls(B):
                ob, db = pv_banks(B)
                O = bank(ob)
                Dn = bank(db)
                jobs = []
                tiles = [3] if B == 1 else [0, 1, 2, 3]
                for j in tiles:
                    n_ = B * 4 + j
                    cs = (lambda T, j=j: T[:, j * 128:(j + 1) * 128])
                    jobs.append((Vt[:, 0, n_ - 1, :], P0[:, n_ - 1 - 6, 128:256], cs, ("P0", n_ - 1)))
                    jobs.append((Vt[:, 0, n_, :], P0[:, n_ - 6, 0:128], cs, ("P0", n_)))
                q0 = QLO1[B]
                for r in range(4):
                    cs = (lambda T, r=r, q0=q0: T.rearrange("p (l r) -> p r l", r=4)[:, r, q0:128])
                    jobs.append((Vt[:, 1, (B - 1) * 4 + r, :], P1[:, (B - 1) % 2, r, 128 + q0:256], cs, ("P1", (B - 1) % 2, r)))
                    jobs.append((Vt[:, 1, B * 4 + r, :], P1[:, B % 2, r, q0:128], cs, ("P1", B % 2, r)))
                ll0 = 24 if B == 1 else 0
                pc0 = {1: 0, 2: 8, 3: 40}[B]
                ncol = 32 - ll0
                for r in range(16):
                    cs = (lambda T, r=r, ll0=ll0: T.rearrange("p (l r) -> p r l", r=16)[:, r, ll0:32])
                    jobs.append((Vt[:, 2, r, :], P2[:, r, pc0:pc0 + ncol], cs, ("P2", r)))
                for i, (vl, pr_, cs, pk) in enumerate(jobs):
                    last = (i == len(jobs) - 1)
                    S.op("pe", lambda e, vl=vl, pr_=pr_, cs=cs, last=last, O=O: e.matmul(cs(O), vl, pr_, start=False, stop=last, skip_group_check=True),
                         reads=["Vt", pk], writes=[("ps", ob)], signal=False)
                    S.op("pe", lambda e, pr_=pr_, cs=cs, last=last, Dn=Dn: e.matmul(cs(Dn), ones_bf, pr_, start=False, stop=last, skip_group_check=True),
                         reads=["cbf", pk], writes=[("ps", db)], signal=last)

            def pv_evac(B, h=h):
                ob, db = pv_banks(B)
                O = bank(ob)
                Dn = bank(db)
                c0 = 384 if B == 1 else 0
                tq = B * 512 + c0 - 896
                ncl = 512 - c0
                S.op("dve", lambda e: e.reciprocal(rden[:, c0:512], Dn[:, c0:512]), reads=[("ps", db)], writes=["rden"])
                S.op("dve", lambda e: e.tensor_tensor(obuf[:, c0:512], O[:, c0:512], rden[:, c0:512], ALU.mult), reads=[("ps", ob), "rden"], writes=["obuf"])
                S.op("dve", lambda e: e.tensor_tensor(ymixT[:, 8 + h, tq:tq + ncl], obuf[:, c0:512], gateT[:, tq:tq + ncl], ALU.mult),
                     reads=["obuf", "gateT"], writes=["ymixT"])

            pv_memset(1)
            pv_memset(2)
            g1_scores(0)
            g1_scores(1)
            pv_matmuls(1)
            g1_scores(2)
            pv_matmuls(2)
            pv_evac(1)
            pv_memset(3)
            g1_scores(3)
            pv_matmuls(3)
            pv_evac(2)
            pv_evac(3)
        S.barrier(skip=[("wo", 1)])

        if stop == 3:
            return finish([(ymixT[:, 8 + f, :], f * 128) for f in range(8)])
        def epi_run(ctx, layer, ymT, ntile, resid_src, dst, end_skip=()):
            Ybuf, wo, ssq, junk2, gpost, xs2 = ctx["Ybuf"], ctx["wo"], ctx["ssq"], ctx["junk2"], ctx["gpost"], ctx["xs2"]
            pref = ctx.get("pref", set())

            def load_res(j):
                if j < ntile:
                    S.dma("sp", xs2[j % 2], resid_src[j * 128:(j + 1) * 128, :], ("x", j % 2), writes=[("xs2", j % 2)])

            def post_a(j):
                p = j % 2
                S.op("dve", lambda e, j=j, p=p: e.tensor_reduce(stat[:, 24 + 2 * p:25 + 2 * p], ssq[:, j * 4:j * 4 + 4], AX.X, ALU.add), reads=[("ssq", j)], writes=[("pst2", p)])
                S.op("dve", lambda e, p=p: e.tensor_scalar(stat[:, 25 + 2 * p:26 + 2 * p], stat[:, 24 + 2 * p:25 + 2 * p], 1.0 / D, EPS, ALU.mult, ALU.add), reads=[("pst2", p)], writes=[("pst2", p)])

            def post_b(j):
                p = j % 2
                xb = xs2[p]
                xr = ("xs2", p)
                rs_c = stat[:, 25 + 2 * p:26 + 2 * p]
                S.op("act", lambda e, rs_c=rs_c: e.activation(rs_c, rs_c, AF.Sqrt), reads=[("pst2", p)], writes=[("pst2", p)])
                S.op("dve", lambda e, rs_c=rs_c: e.reciprocal(rs_c, rs_c), reads=[("pst2", p)], writes=[("pst2", p)])
                Yj = Ybuf[:, j, :]
                S.op("dve", lambda e, Yj=Yj, xb=xb, rs_c=rs_c: e.scalar_tensor_tensor(Yj, Yj, rs_c, xb, ALU.mult, ALU.add), reads=[("Y", j), ("pst2", p), xr], writes=[("Y", j)])
                load_res(j + 2)
                if dst(j) is not None:
                    S.dma("sp", dst(j), Yj, "o", reads=[("Y", j)])

            if "res" not in pref:
                S.dma("sp", gpost, gvec[1 + 2 * layer], "c", writes=["gpost"])
                load_res(0)
                load_res(1)
            for c in range(4):
                wb = wo[c % 2]
                if c not in pref:
                    S.dma("pool", wb.rearrange("p k c -> p (k c)"), woseq[layer * 4 + c], ("wo", c % 2), writes=[("wo", c % 2)])
                for j in range(ntile):
                    b = accst["i"] % 4
                    accst["i"] += 1
                    acc = bank(b)
                    for kc in range(16):
                        S.op("pe", lambda e, acc=acc, kc=kc, j=j, wb=wb: e.matmul(acc, ymT[:, kc, j * 128:(j + 1) * 128], wb[:, kc, :], start=(kc == 0), stop=(kc == 15)),
                             reads=[("wo", c % 2), "ymT"], writes=[("ps", b)], signal=(kc == 15))
                    S.op("act", lambda e, acc=acc, j=j, c=c: e.activation(junk2, acc, AF.Square, accum_out=ssq[:, j * 4 + c:j * 4 + c + 1]),
                         reads=[("ps", b)], writes=["junk2", ("ssq", j), ("psr", b)])
                    S.op("dve", lambda e, acc=acc, j=j, c=c: e.tensor_tensor(Ybuf[:, j, c * 512:(c + 1) * 512], acc, gpost[:, c * 512:(c + 1) * 512], ALU.mult),
                         reads=[("ps", b), ("psr", b), "gpost"], writes=[("Y", j)])
                    if c == 3:
                        if j >= 1:
                            post_a(j - 1)
                        if j >= 2:
                            post_b(j - 2)
            post_a(ntile - 1)
            post_b(ntile - 2)
            post_b(ntile - 1)
            S.barrier(skip=end_skip)

        if stop == 4:
            epi_run(EP0, 0, ymixT, 9, xw[896:2048, :], lambda j: (out[(j - 1) * 128:j * 128, :] if j >= 1 else None))
            S._emit_waits("sp", [["o", S.dcnt["o"]]])
            S.emit()
            return nc
        gB_l1 = arena_t[:, ARENA - 2048:ARENA]
        S.dma("sp", gB_l1, gvec[2], "c", writes=["gB_l1"])
        wstate["cap"] = 144
        w_issue(96 + NSLOT - 1)
        epi_run(EP0, 0, ymixT, 9, xw[896:2048, :], lambda j: x1s[j * 128:(j + 1) * 128, :], end_skip=["o"])

        L1r = top.sub()
        h1T = L1r.bf(16 * 1152).rearrange("p (k t) -> p k t", k=16)
        prenorm_phase(L1r, x1s, 9, 2, h1T, sb_tiles=[EP0["Ybuf"][:, j, :] for j in range(9)], gB_pre=gB_l1)
        L1 = L1r.sub()
        wo_end = arena_t[:, ARENA - 4096:ARENA].bitcast(BF16).rearrange("p (k c) -> p k c", k=16)
        EP1 = epi_alloc(8, wo0=wo_end)
        ym1 = ymixT.rearrange("p f t -> p (f t)")[:, 0:16 * 1024].rearrange("p (f t) -> p f t", f=16)
        OWN = [(128, 512), (640, 512)]
        HAL = [(0, 128), (128, 512), (640, 512)]

        SG = L1.sub()
        vbuf = SG.f32(8 * 1024).rearrange("p (n c) -> p n c", n=8)
        vn = SG.bf(8 * 1024).rearrange("p (n c) -> p n c", n=8)
        vt1 = SG.f32(1024)
        vt2 = SG.f32(1024)
        sgc = SG.f32(2560)
        wsf = SG.f32(512)
        wsb = SG.bf(512).rearrange("p (h i) -> p h i", h=4)
        uTs = [SG.f32(1024) for _ in range(3)]
        sgTs = [SG.f32(1024) for _ in range(3)]
        st1 = SG.f32(128)
        st2 = SG.f32(128)
        junk4 = SG.bf(1024)
        S.dma("sp", sgc, sguc, "c", writes=["sgc"])
        S.dma("sp", wsf, wsT_d, "c", writes=["wsf"])
        for hh in range(4):
            S.op("dve", lambda e, hh=hh: e.tensor_tensor(wsb[:, hh, :], wsf[:, hh * 128:(hh + 1) * 128], m_std[:, 0:128], ALU.mult),
                 reads=["wsf", "cbf"], writes=["wsb"])
        for ct in range(8):
            s = w_take()
            for n4 in range(2):
                b = ACC_BANKS[accst["i"] % len(ACC_BANKS)]
                accst["i"] += 1
                acc = bank(b)
                for nn in range(4):
                    n_ = n4 * 4 + nn
                    for kc in range(16):
                        S.op("pe", lambda e, acc=acc, nn=nn, n_=n_, kc=kc, s=s: e.matmul(
                            acc[:, nn * 128:(nn + 1) * 128], h1T[:, kc, 128 + n_ * 128:128 + (n_ + 1) * 128], ring[s][:, kc, :], start=(kc == 0), stop=(kc == 15)),
                            reads=[("w", s), "hT"], writes=[("ps", b)], signal=(kc == 15 and nn == 3))
                S.op("act", lambda e, acc=acc, n4=n4, ct=ct: e.copy(vbuf[:, n4 * 4:n4 * 4 + 4, ct * 128:(ct + 1) * 128], acc.rearrange("p (n c) -> p n c", n=4)),
                     reads=[("ps", b)], writes=["vbuf"])
            w_done()
        def ln_tile(n_):
            vv = vbuf[:, n_, :]
            S.op("act", lambda e, vv=vv: e.activation(junk4, vv, AF.Copy, accum_out=stat[:, 8:9]), reads=["vbuf"], writes=["junk4", "stat"])
            S.op("act", lambda e, vv=vv: e.activation(junk4, vv, AF.Square, accum_out=stat[:, 9:10]), reads=["vbuf"], writes=["junk4", "stat"])
            S.op("dve", lambda e: e.tensor_scalar(stat[:, 10:11], stat[:, 8:9], 1.0 / 1024, None, ALU.mult), reads=["stat"], writes=["stat"])
            S.op("dve", lambda e: e.tensor_tensor(stat[:, 11:12], stat[:, 10:11], stat[:, 10:11], ALU.mult), reads=["stat"], writes=["stat"])
            S.op("dve", lambda e: e.scalar_tensor_tensor(stat[:, 12:13], stat[:, 9:10], 1.0 / 1024, stat[:, 11:12], ALU.mult, ALU.subtract), reads=["stat"], writes=["stat"])
            S.op("dve", lambda e: e.tensor_scalar(stat[:, 13:14], stat[:, 12:13], EPS, None, ALU.add), reads=["stat"], writes=["stat"])
            S.op("act", lambda e: e.activation(stat[:, 13:14], stat[:, 13:14], AF.Sqrt), reads=["stat"], writes=["stat"])
            S.op("dve", lambda e: e.reciprocal(stat[:, 13:14], stat[:, 13:14]), reads=["stat"], writes=["stat"])
            S.op("dve", lambda e, vv=vv: e.tensor_scalar(vt1, vv, stat[:, 10:11], stat[:, 13:14], ALU.subtract, ALU.mult), reads=["vbuf", "stat"], writes=["vt1"])
            S.op("dve", lambda e: e.tensor_tensor(vt2, vt1, sgc[:, 0:1024], ALU.mult), reads=["vt1", "sgc"], writes=["vt2"])
            S.op("dve", lambda e, n_=n_: e.tensor_tensor(vn[:, n_, :], vt2, sgc[:, 1024:2048], ALU.add), reads=["vt2", "sgc"], writes=["vn"])

        for n2 in range(0, 8, 2):
            fillers.append(lambda n2=n2: (ln_tile(n2), ln_tile(n2 + 1)))
        def sgu_project(ct):
            uT, sgT, kk = uTs[ct % 3], sgTs[ct % 3], ct % 3

            def evac_u(acc, pr, t0, n):
                S.op("act", lambda e: e.copy(uT[:, t0 - 128:t0 - 128 + n], acc), reads=[pr], writes=[("uT", kk)])
            inproj(h1T, OWN, evac_u)

            def evac_cg(acc, pr, t0, n):
                S.op("act", lambda e: e.activation(sgT[:, t0 - 128:t0 - 128 + n], acc, AF.Silu), reads=[pr], writes=[("sgT", kk)])
            inproj(h1T, OWN, evac_cg)

        def sgu_spatial(ct):
            uT, sgT, kk = uTs[ct % 3], sgTs[ct % 3], ct % 3
            hh = ct // 2
            for n4 in range(2):
                pb = 4 + n4
                sp_ = bank(pb)
                for nn in range(4):
                    n_ = n4 * 4 + nn
                    S.op("pe", lambda e, sp_=sp_, nn=nn, n_=n_: e.matmul(sp_[:, nn * 128:(nn + 1) * 128], vn[:, n_, ct * 128:(ct + 1) * 128], wsb[:, hh, :], start=True, stop=True),
                         reads=["vn", "wsb"], writes=[("ps", pb)], signal=(nn == 3))
                for nn in range(4):
                    n_ = n4 * 4 + nn
                    k = nn % 2
                    stt, stk_ = (st1, "st1") if k == 0 else (st2, "st2")
                    S.op("dve", lambda e, sp_=sp_, nn=nn, stt=stt: e.tensor_tensor(stt, sp_[:, nn * 128:(nn + 1) * 128], sgc[:, 2048 + hh * 128:2048 + (hh + 1) * 128], ALU.add),
                         reads=[("ps", pb), "sgc"], writes=[stk_])
                    S.op("dve", lambda e, stt=stt, n_=n_: e.tensor_tensor(stt, stt, uT[:, n_ * 128:(n_ + 1) * 128], ALU.mult), reads=[stk_, ("uT", kk)], writes=[stk_])
                    S.op("dve", lambda e, stt=stt, n_=n_: e.tensor_tensor(ym1[:, ct, n_ * 128:(n_ + 1) * 128], stt, sgT[:, n_ * 128:(n_ + 1) * 128], ALU.mult),
                         reads=[stk_, ("sgT", kk)], writes=["ym1"])

        for step in range(8 + 2):
            if step < 8:
                sgu_project(step)
            if step == 1:
                while fillers:
                    fillers.pop(0)()
            if step >= 2:
                sgu_spatial(step - 2)
        S.barrier()

        if stop == 6:
            return finish([(ym1[:, f, :], f * 128) for f in range(8)])
        CV = L1.sub()
        sigs = [CV.f32(1152), CV.f32(1152)]
        dbfs = [CV.bf(1152), CV.bf(1152)]
        DGs = [CV.bf(31 * 128).rearrange("p (k c) -> p k c", k=31) for _ in range(2)]
        convb = CV.f32(8 * 1024).rearrange("p (c t) -> p c t", c=8)
        sqt = [CV.f32(512), CV.f32(512)]
        MU = CV.f32(1024)
        RS = CV.f32(1024)
        sdg = [CV.f32(512), CV.f32(512)]
        ct1 = [CV.f32(512), CV.f32(512)]
        ct2 = [CV.f32(512), CV.f32(512)]
        cvst = {"i": 0}
        for ct in range(8):
            pp = ct % 2
            sig, dbf, DG = sigs[pp], dbfs[pp], DGs[pp]
            for k in range(31):
                S.op("dve", lambda e, DG=DG, k=k, ct=ct: e.tensor_scalar(DG[:, k, :], ident, pcol[:, 32 + ct * 31 + k:32 + ct * 31 + k + 1], None, ALU.mult),
                     reads=["cbf", "pcol"], writes=[("DG", pp)])

            if ct == 4:
                epi_prefetch(EP1, 1, [0])

            def evac_glu(acc, pr, t0, n, sig=sig, pp=pp):
                S.op("act", lambda e: e.activation(sig[:, t0:t0 + n], acc, AF.Sigmoid), reads=[pr], writes=[("sig", pp)])
            inproj(h1T, HAL, evac_glu)

            def conv_mm(ct=ct, pp=pp, dbf=dbf, DG=DG):
                for c2 in range(2):
                    cb = 2 + (cvst["i"] % 2)
                    cvst["i"] += 1
                    cacc = bank(cb)
                    for k in range(31):
                        S.op("pe", lambda e, cacc=cacc, k=k, c2=c2: e.matmul(cacc, DG[:, k, :], dbf[:, 98 + k + c2 * 512:98 + k + c2 * 512 + 512], start=(k == 0), stop=(k == 30)),
                             reads=[("DG", pp), ("dbuf", pp)], writes=[("ps", cb)], signal=(k == 30))
                    S.op("dve", lambda e, cacc=cacc, c2=c2: e.tensor_scalar(convb[:, ct, c2 * 512:(c2 + 1) * 512], cacc, pcol[:, 8 + ct:9 + ct], None, ALU.add),
                         reads=[("ps", cb), "pcol"], writes=[("conv", ct)])

            def evac_val(acc, pr, t0, n, sig=sig, dbf=dbf, pp=pp, conv_mm=conv_mm):
                S.op("dve", lambda e: e.tensor_tensor(dbf[:, t0:t0 + n], acc, sig[:, t0:t0 + n], ALU.mult), reads=[pr, ("sig", pp)], writes=[("dbuf", pp)])
                if t0 == 640:
                    deferred.append(conv_mm)
            inproj(h1T, HAL, evac_val)
        flush_deferred()
        for c2 in range(2):
            mu_ps = bank(4 + c2)
            sq_ps = bank(6 + c2)
            for ct in range(8):
                S.op("pe", lambda e, mu_ps=mu_ps, ct=ct, c2=c2: e.matmul(mu_ps, ones_f, convb[:, ct, c2 * 512:(c2 + 1) * 512], start=(ct == 0), stop=(ct == 7)),
                     reads=[("conv", ct), "cf32"], writes=[("ps", 4 + c2)], signal=(ct == 7))
            for ct in range(8):
                k = ct % 2
                S.op("act", lambda e, ct=ct, c2=c2, k=k: e.activation(sqt[k], convb[:, ct, c2 * 512:(c2 + 1) * 512], AF.Square), reads=[("conv", ct)], writes=[("sqt", k)])
                S.op("pe", lambda e, sq_ps=sq_ps, ct=ct, k=k: e.matmul(sq_ps, ones_f, sqt[k], start=(ct == 0), stop=(ct == 7)),
                     reads=[("sqt", k), "cf32"], writes=[("ps", 6 + c2)], signal=True)
            mu = MU[:, c2 * 512:(c2 + 1) * 512]
            rs_ = RS[:, c2 * 512:(c2 + 1) * 512]
            S.op("dve", lambda e, mu=mu, mu_ps=mu_ps: e.tensor_scalar(mu, mu_ps, 1.0 / 1024, None, ALU.mult), reads=[("ps", 4 + c2)], writes=["MU"])
            S.op("dve", lambda e, rs_=rs_, mu=mu: e.tensor_tensor(rs_, mu, mu, ALU.mult), reads=["MU"], writes=["RS"])
            S.op("dve", lambda e, rs_=rs_, sq_ps=sq_ps: e.scalar_tensor_tensor(rs_, sq_ps, 1.0 / 1024, rs_, ALU.mult, ALU.subtract), reads=[("ps", 6 + c2), "RS"], writes=["RS"])
            S.op("dve", lambda e, rs_=rs_: e.tensor_scalar(rs_, rs_, EPS, None, ALU.add), reads=["RS"], writes=["RS"])
            S.op("act", lambda e, rs_=rs_: e.activation(rs_, rs_, AF.Sqrt), reads=["RS"], writes=["RS"])
            S.op("dve", lambda e, rs_=rs_: e.reciprocal(rs_, rs_), reads=["RS"], writes=["RS"])
        for ct in range(8):
            def evac_dg(acc, pr, t0, n, ct=ct):
                c0 = t0 - 128
                k = (c0 // 512) % 2
                S.op("act", lambda e: e.activation(sdg[k], acc, AF.Silu), reads=[pr], writes=[("sdg", k)])
                S.op("dve", lambda e: e.tensor_tensor(ct1[k], convb[:, ct, c0:c0 + n], MU[:, c0:c0 + n], ALU.subtract), reads=[("conv", ct), "MU"], writes=[("ct1", k)])
                S.op("dve", lambda e: e.tensor_tensor(ct1[k], ct1[k], RS[:, c0:c0 + n], ALU.mult), reads=[("ct1", k), "RS"], writes=[("ct1", k)])
                S.op("act", lambda e: e.activation(ct2[k], ct1[k], AF.Silu, bias=pcol[:, 24 + ct:25 + ct], scale=pcol[:, 16 + ct:17 + ct]),
                     reads=[("ct1", k), "pcol"], writes=[("ct2", k)])
                S.op("dve", lambda e: e.tensor_tensor(ym1[:, 8 + ct, c0:c0 + n], ct2[k], sdg[k], ALU.mult), reads=[("ct2", k), ("sdg", k)], writes=["ym1"])
            inproj(h1T, OWN, evac_dg)
        S.barrier()

        if stop == 7:
            return finish([(ym1[:, 8 + f, :], f * 128) for f in range(8)])
        epi_run(EP1, 1, ym1, 8, x1s[128:1152, :], lambda j: out[j * 128:(j + 1) * 128, :])
        S._emit_waits("sp", [["o", S.dcnt["o"]]])
        S.emit()
    return nc


_NC_CACHE = {}


def _tile_w(w, cols):
    sub = w[:, cols]
    return sub.reshape(16, 128, 128).transpose(1, 0, 2).reshape(128, 2048)


def kernel(x, e_pre_norm, e_w_in, e_pool_w, e_pool_scale, e_w_out, e_post_norm,
           o_pre_norm, o_w_in, o_sgu_norm_g, o_sgu_norm_b, o_sgu_w, o_sgu_b,
           o_conv_w, o_conv_b, o_conv_norm_g, o_conv_norm_b, o_w_out, o_post_norm):
    f = lambda a: np.ascontiguousarray(np.asarray(a, dtype=np.float32))
    x = f(x)
    w0 = f(e_w_in)[0]
    w1 = f(o_w_in)[0]
    tiles = []
    ar = np.arange(128)
    for g in range(4):
        for j in range(2):
            tiles.append(_tile_w(w0, g * 256 + j * 128 + ar))
        for j in range(2):
            tiles.append(_tile_w(w0, 1024 + g * 256 + j * 128 + ar))
    for h in range(8):
        for g in range(3):
            tiles.append(_tile_w(w0, 5120 + g * 1024 + h * 128 + ar))
        for g in range(3):
            tiles.append(_tile_w(w0, 8192 + g * 1024 + h * 128 + ar))
        for g in range(3):
            tiles.append(_tile_w(w0, 2048 + g * 1024 + h * 128 + ar))
        tiles.append(_tile_w(w0, 11264 + h * 128 + ar))
    for ct in range(8):
        tiles.append(_tile_w(w1, 1024 + ct * 128 + ar))
    for ct in range(8):
        tiles.append(_tile_w(w1, 0 + ct * 128 + ar))
        tiles.append(_tile_w(w1, 2048 + ct * 128 + ar))
    for ct in range(8):
        tiles.append(_tile_w(w1, 4096 + ct * 128 + ar))
        tiles.append(_tile_w(w1, 3072 + ct * 128 + ar))
    for ct in range(8):
        tiles.append(_tile_w(w1, 5120 + ct * 128 + ar))
    wseq = np.ascontiguousarray(np.stack(tiles, 0))
    assert wseq.shape == (144, 128, 2048)
    wos = []
    for wo in (f(e_w_out)[0], f(o_w_out)[0]):
        for c in range(4):
            sub = wo[:, c * 512:(c + 1) * 512]
            wos.append(sub.reshape(16, 128, 512).transpose(1, 0, 2).reshape(128, 8192))
    woseq = np.ascontiguousarray(np.stack(wos, 0))
    bc = lambda v: np.broadcast_to(f(v).reshape(1, -1), (128, f(v).size))
    gvec = np.ascontiguousarray(np.stack([bc(e_pre_norm[0]), bc(e_post_norm[0]), bc(o_pre_norm[0]), bc(o_post_norm[0])], 0))
    sguc = np.ascontiguousarray(np.concatenate([bc(o_sgu_norm_g[0]), bc(o_sgu_norm_b[0]), bc(f(o_sgu_b)[0].reshape(-1))], 1))
    wsT = np.ascontiguousarray(f(o_sgu_w)[0].transpose(2, 0, 1).reshape(128, 512))
    colv = lambda v: f(v).reshape(8, 128).T
    cw = f(o_conv_w)[0]
    cwt = cw.reshape(31, 8, 128).transpose(2, 1, 0).reshape(128, 248)
    pcol = np.ascontiguousarray(np.concatenate([colv(e_pool_scale[0]), colv(o_conv_b[0]), colv(o_conv_norm_g[0]), colv(o_conv_norm_b[0]), cwt], 1))
    poolw = np.ascontiguousarray(f(e_pool_w)[0].reshape(4, 2, 128, 256).transpose(0, 2, 1, 3).reshape(4, 128, 512))
    kk = np.arange(128)[:, None]
    qq = np.arange(128)[None, :]
    ident = np.eye(128, dtype=np.float32)
    ones = np.ones((128, 128), np.float32)
    psw = np.zeros((128, 32), np.float32)
    for m in range(32):
        psw[(m + 16) % 32, m] = 1.0
    m_own = (kk <= qq).astype(np.float32)
    m_next = (kk >= qq).astype(np.float32)
    m_std = np.concatenate([m_own, m_next], 1)
    lq = np.arange(56, 128)[None, :]
    half = 16
    inv_freq = np.power(np.float32(500000.0), -np.arange(0, 32, 2, dtype=np.float32) / np.float32(32)).astype(np.float32)
    in_maps = []
    for c in range(NCORES):
        b, hf = c // 2, c % 2
        if hf == 1:
            xwin = x[b]
            pos = np.arange(2048, dtype=np.float32)
            m_bnd = m_std
            m_g2 = (kk <= lq).astype(np.float32)
            invfix = np.concatenate([np.full((16,), 1.0 / w, np.float32) for w in (2, 4, 8, 16)])
        else:
            xwin = np.concatenate([np.zeros((1024, D), np.float32), x[b, :1024]], 0)
            pos = np.maximum(np.arange(2048, dtype=np.float32) - 1024, 0).astype(np.float32)
            m_bnd = np.concatenate([m_own, np.zeros((128, 128), np.float32)], 1)
            m_g2 = ((kk <= lq) & (kk >= 64)).astype(np.float32)
            tt = np.arange(16, dtype=np.float32)
            invfix = np.concatenate([1.0 / np.minimum(tt + 1, w) for w in (2, 4, 8, 16)]).astype(np.float32)
        ang = (pos[:, None] * inv_freq[None, :]).astype(np.float32)
        cs, sn = np.cos(ang).astype(np.float32).T, np.sin(ang).astype(np.float32).T
        rope = np.concatenate([np.concatenate([cs, cs], 0), np.concatenate([-sn, sn], 0)], 1)
        cbf = np.concatenate([ident, ones, psw, m_std, m_bnd, m_g2], 1)
        cf32 = np.concatenate([np.broadcast_to(invfix[None, :], (128, 64)), ones], 1)
        in_maps.append(dict(
            xw=np.ascontiguousarray(xwin), wseq=wseq, woseq=woseq, gvec=gvec, sguc=sguc, wsT=wsT, pcol=pcol,
            poolw=poolw, rope=np.ascontiguousarray(rope.astype(np.float32)), cbf=np.ascontiguousarray(cbf.astype(np.float32)),
            cf32=np.ascontiguousarray(cf32.astype(np.float32))))
    if "nc" not in _NC_CACHE:
        _NC_CACHE["nc"] = build_nc()
    res = run_bass_kernel_spmd(_NC_CACHE["nc"], in_maps, core_ids=list(range(NCORES)))
    outp = np.empty((4, 2048, D), np.float32)
    for c in range(NCORES):
        b, hf = c // 2, c % 2
        outp[b, hf * 1024:(hf + 1) * 1024] = np.asarray(res.results[c]["out"], dtype=np.float32)
    return outp
```
